# Optimizing a Trainium2 kernel written in Bass

```python
import math
import jax, jax.numpy as jnp
from jax import lax
import numpy as np

D_MODEL = 1024
BATCH = 16
SEQ = 2048
DEPTH = 4
DEC_BATCH = 8
DEC_SEQ = 8192
PAST_LEN = 128

N_MIXERS = 2
N_RWKV = (DEPTH + 1) // 2
N_HGRN = DEPTH // 2
RW_HEAD = 64
RW_HEADS = D_MODEL // RW_HEAD
RW_DECAY_LORA = max(32, int(round(1.8 * D_MODEL ** 0.5 / 32)) * 32)
RW_AAA_LORA = max(32, int(round(1.8 * D_MODEL ** 0.5 / 32)) * 32)
RW_MV_LORA = max(32, int(round(1.3 * D_MODEL ** 0.5 / 32)) * 32)
RW_GATE_LORA = max(32, int(round(0.6 * D_MODEL ** 0.8 / 32)) * 32)
RW_LN_EPS = 64e-5
HG_EXPAND = 128
HG_HEADS = D_MODEL // HG_EXPAND
HG_HEAD_V = D_MODEL // HG_HEADS
HG_CHUNK = 64
FFN_HIDDEN = int(math.ceil(8 * D_MODEL / 3 / 256)) * 256
NORM_EPS = 1e-6

kernel_name = 'bi_rwkv7_hgrn2_hybrid_encoder'


def _rmsnorm(x, w):
    xf = x.astype(jnp.float32)
    y = xf * lax.rsqrt(jnp.mean(xf * xf, axis=-1, keepdims=True) + NORM_EPS)
    return (y * w.astype(jnp.float32)).astype(x.dtype)


def _swiglu(x, w_in, w_out):
    gate, up = jnp.split(x @ w_in, 2, axis=-1)
    return (jax.nn.silu(gate) * up) @ w_out


def _bi_token_shift(x):
    half = x.shape[-1] // 2
    prev = jnp.pad(x[:, :-1, :half], ((0, 0), (1, 0), (0, 0)))
    nxt = jnp.pad(x[:, 1:, half:], ((0, 0), (0, 1), (0, 0)))
    return jnp.concatenate([prev, nxt], axis=-1)


def _rwkv7_scan(r, w, k, v, a, b, reverse):
    bsz, _, h, n = r.shape
    tm = lambda t: jnp.swapaxes(t, 0, 1)

    def step(S, inp):
        r_t, w_t, k_t, v_t, a_t, b_t = inp
        sa = jnp.einsum('bhvk,bhk->bhv', S, a_t)
        S = S * w_t[:, :, None, :] + sa[..., None] * b_t[:, :, None, :] + v_t[..., None] * k_t[:, :, None, :]
        return S, jnp.einsum('bhvk,bhk->bhv', S, r_t)

    S0 = jnp.zeros((bsz, h, n, n), jnp.float32)
    _, o = lax.scan(step, S0, (tm(r), tm(w), tm(k), tm(v), tm(a), tm(b)), reverse=reverse)
    return tm(o)


def _rwkv7_mixer(x, v_first, v_res, mu, w_rkv, w0, w1, w2, a0, a1, a2, g1, g2,
                 k_k, k_a, r_k, lnx_w, lnx_b, w_o):
    bsz, t, d = x.shape
    h, n = RW_HEADS, RW_HEAD
    f32 = jnp.float32
    xx = _bi_token_shift(x) - x
    mix = lambda i: x + xx * mu[i]
    xv = mix(3)
    r = mix(0) @ w_rkv[0]
    k = mix(2) @ w_rkv[1]
    v = xv @ w_rkv[2]
    if v_res is None:
        v_first = v
    else:
        v0, v1, v2 = v_res
        v = v + (v_first - v) * jax.nn.sigmoid(v0 + (xv @ v1) @ v2)
    g = jax.nn.sigmoid(mix(5) @ g1) @ g2
    heads = lambda z: z.astype(f32).reshape(bsz, t, h, n)
    kk = heads(k * k_k)
    kk = kk / jnp.maximum(jnp.sqrt(jnp.sum(kk * kk, axis=-1, keepdims=True)), 1e-12)
    rh, kh, vh = heads(r), heads(k), heads(v)
    k_a_h = k_a.astype(f32).reshape(h, n)
    r_k_h = r_k.astype(f32).reshape(h, n)
    xw, xa = mix(1), mix(4)
    outs = []
    bonus = []
    for di, rev in ((0, False), (1, True)):
        wl = heads(w0[di] + jnp.tanh(xw @ w1[di]) @ w2[di])
        logw = -jax.nn.softplus(-wl) - 0.5
        decay = jnp.exp(-jnp.exp(logw))
        a = jax.nn.sigmoid(heads(a0[di] + (xa @ a1[di]) @ a2[di]))
        kd = kh * (1.0 + (a - 1.0) * k_a_h)
        outs.append(_rwkv7_scan(rh, decay, kd, vh, -kk, kk * a, rev))
        bonus.append(jnp.sum(rh * kd * r_k_h, axis=-1, keepdims=True) * vh)
    o = outs[0] + outs[1]
    m = jnp.mean(o, axis=-1, keepdims=True)
    var = jnp.mean((o - m) ** 2, axis=-1, keepdims=True)
    y = ((o - m) * lax.rsqrt(var + RW_LN_EPS)).reshape(bsz, t, d)
    y = y * lnx_w.astype(f32) + lnx_b.astype(f32) + (bonus[0] + bonus[1]).reshape(bsz, t, d)
    return (y.astype(x.dtype) * g) @ w_o, v_first


def _hgrn2_chunk_scan(q, k, v, g):
    bsz, t, h, kd = q.shape
    vd = v.shape[-1]
    c = HG_CHUNK
    nc = t // c
    chunks = lambda z: z.reshape(bsz, nc, c, h, z.shape[-1]).transpose(1, 0, 3, 2, 4)
    mask = jnp.tril(jnp.ones((c, c), dtype=bool))[None, None, :, :, None]

    def step(S, inp):
        qc, kc, vc, gc = inp
        G = jnp.cumsum(gc, axis=2)
        decay = jnp.exp(jnp.where(mask, G[:, :, :, None, :] - G[:, :, None, :, :], -jnp.inf))
        A = jnp.einsum('bhijd,bhjd->bhij', qc[:, :, :, None, :] * decay, kc)
        o = jnp.einsum('bhij,bhjv->bhiv', A, vc) + jnp.einsum('bhid,bhdv->bhiv', qc * jnp.exp(G), S)
        g_last = G[:, :, -1, :]
        S = jnp.exp(g_last)[..., None] * S + jnp.einsum(
            'bhjd,bhjv->bhdv', kc * jnp.exp(g_last[:, :, None, :] - G), vc)
        return S, o

    S0 = jnp.zeros((bsz, h, kd, vd), jnp.float32)
    _, o = lax.scan(step, S0, (chunks(q), chunks(k), chunks(v), chunks(g)))
    return o.transpose(1, 0, 3, 2, 4).reshape(bsz, t, h, vd)


def _hgrn2_mixer(x, w_in, lb, gnorm, w_o):
    bsz, t, d = x.shape
    h, kd, vd = HG_HEADS, HG_EXPAND, HG_HEAD_V
    f32 = jnp.float32
    q, inp, gate, f_fwd, f_bwd = jnp.split(x @ w_in, 5, axis=-1)
    heads_k = lambda z: z.astype(f32).reshape(bsz, t, h, kd)
    qh = heads_k(jax.nn.silu(q))
    vh = inp.astype(f32).reshape(bsz, t, h, vd)
    lb = lb.astype(f32).reshape(h, kd)
    log_lb = jnp.log(lb)
    log_1mlb = jnp.log1p(-lb)
    outs = []
    for f, rev in ((f_fwd, False), (f_bwd, True)):
        z = heads_k(f)
        logf = jnp.logaddexp(log_lb, log_1mlb + jax.nn.log_sigmoid(z))
        kh = (1.0 - lb) * jax.nn.sigmoid(-z)
        flip = (lambda a: jnp.flip(a, axis=1)) if rev else (lambda a: a)
        outs.append(flip(_hgrn2_chunk_scan(flip(qh), flip(kh), flip(vh), flip(logf))))
    o = outs[0] + outs[1]
    o = o * lax.rsqrt(jnp.mean(o * o, axis=-1, keepdims=True) + NORM_EPS)
    o = (o * gnorm.astype(f32).reshape(h, vd)).reshape(bsz, t, d).astype(x.dtype)
    return (o * jax.nn.silu(gate)) @ w_o


def _trunk(x, norm_mix, norm_ffn, norm_final, rw_mu, rw_w_rkv, rw_w0, rw_w1, rw_w2,
           rw_a0, rw_a1, rw_a2, rw_v0, rw_v1, rw_v2, rw_g1, rw_g2, rw_k_k, rw_k_a, rw_r_k,
           rw_lnx_w, rw_lnx_b, rw_w_o, hg_w_in, hg_lb_logits, hg_gnorm, hg_w_o,
           ffn_w_in, ffn_w_out):
    p = jax.nn.softmax(hg_lb_logits.astype(jnp.float32), axis=0)
    lb_all = jnp.cumsum(p, axis=0) - p[0]
    v_first = None
    for i in range(DEPTH):
        hin = _rmsnorm(x, norm_mix[i])
        j = i // N_MIXERS
        if i % N_MIXERS == 0:
            v_res = None if j == 0 else (rw_v0[j - 1], rw_v1[j - 1], rw_v2[j - 1])
            out, v_first = _rwkv7_mixer(hin, v_first, v_res, rw_mu[j], rw_w_rkv[j], rw_w0[j], rw_w1[j],
                                        rw_w2[j], rw_a0[j], rw_a1[j], rw_a2[j], rw_g1[j], rw_g2[j],
                                        rw_k_k[j], rw_k_a[j], rw_r_k[j], rw_lnx_w[j], rw_lnx_b[j], rw_w_o[j])
        else:
            out = _hgrn2_mixer(hin, hg_w_in[j], lb_all[i], hg_gnorm[j], hg_w_o[j])
        x = x + out.astype(x.dtype)
        x = x + _swiglu(_rmsnorm(x, norm_ffn[i]), ffn_w_in[i], ffn_w_out[i]).astype(x.dtype)
    return _rmsnorm(x, norm_final)


def setup_inputs(seed: int = 0) -> dict:
    key = jax.random.key(seed)
    ks = iter(jax.random.split(key, 40))
    nrm = lambda shape, scale: scale * jax.random.normal(next(ks), shape, jnp.float32)
    D = D_MODEL
    return {
        'x_prompt': nrm((BATCH, SEQ, D), 1.0),
        'x_sample': nrm((DEC_BATCH, DEC_SEQ, D), 1.0),
        'norm_mix': 1.0 + nrm((DEPTH, D), 0.01),
        'norm_ffn': 1.0 + nrm((DEPTH, D), 0.01),
        'norm_final': 1.0 + nrm((D,), 0.01),
        'rw_mu': jax.random.uniform(next(ks), (N_RWKV, 6, D), jnp.float32),
        'rw_w_rkv': nrm((N_RWKV, 3, D, D), D ** -0.5),
        'rw_w0': -1.0 + nrm((N_RWKV, 2, D), 0.5),
        'rw_w1': nrm((N_RWKV, 2, D, RW_DECAY_LORA), D ** -0.5),
        'rw_w2': nrm((N_RWKV, 2, RW_DECAY_LORA, D), 0.1 * RW_DECAY_LORA ** -0.5),
        'rw_a0': nrm((N_RWKV, 2, D), 0.1),
        'rw_a1': nrm((N_RWKV, 2, D, RW_AAA_LORA), D ** -0.5),
        'rw_a2': nrm((N_RWKV, 2, RW_AAA_LORA, D), 0.1 * RW_AAA_LORA ** -0.5),
        'rw_v0': 1.0 + nrm((N_RWKV - 1, D), 0.1),
        'rw_v1': nrm((N_RWKV - 1, D, RW_MV_LORA), D ** -0.5),
        'rw_v2': nrm((N_RWKV - 1, RW_MV_LORA, D), 0.1 * RW_MV_LORA ** -0.5),
        'rw_g1': nrm((N_RWKV, D, RW_GATE_LORA), D ** -0.5),
        'rw_g2': nrm((N_RWKV, RW_GATE_LORA, D), RW_GATE_LORA ** -0.5),
        'rw_k_k': 0.85 + nrm((N_RWKV, D), 0.05),
        'rw_k_a': 1.0 + nrm((N_RWKV, D), 0.05),
        'rw_r_k': nrm((N_RWKV, D), 0.1),
        'rw_lnx_w': 1.0 + nrm((N_RWKV, D), 0.01),
        'rw_lnx_b': nrm((N_RWKV, D), 0.01),
        'rw_w_o': nrm((N_RWKV, D, D), D ** -0.5),
        'hg_w_in': nrm((N_HGRN, D, 5 * D), D ** -0.5),
        'hg_lb_logits': nrm((DEPTH, D), 0.1),
        'hg_gnorm': 1.0 + nrm((N_HGRN, D), 0.01),
        'hg_w_o': nrm((N_HGRN, D, D), D ** -0.5),
        'ffn_w_in': nrm((DEPTH, D, 2 * FFN_HIDDEN), D ** -0.5),
        'ffn_w_out': nrm((DEPTH, FFN_HIDDEN, D), FFN_HIDDEN ** -0.5),
    }


def reference(x_prompt, x_sample, norm_mix, norm_ffn, norm_final, rw_mu, rw_w_rkv, rw_w0, rw_w1,
              rw_w2, rw_a0, rw_a1, rw_a2, rw_v0, rw_v1, rw_v2, rw_g1, rw_g2, rw_k_k, rw_k_a,
              rw_r_k, rw_lnx_w, rw_lnx_b, rw_w_o, hg_w_in, hg_lb_logits, hg_gnorm, hg_w_o,
              ffn_w_in, ffn_w_out):
    y_prompt = _trunk(x_prompt, norm_mix, norm_ffn, norm_final, rw_mu, rw_w_rkv, rw_w0, rw_w1, rw_w2,
                      rw_a0, rw_a1, rw_a2, rw_v0, rw_v1, rw_v2, rw_g1, rw_g2, rw_k_k, rw_k_a, rw_r_k,
                      rw_lnx_w, rw_lnx_b, rw_w_o, hg_w_in, hg_lb_logits, hg_gnorm, hg_w_o,
                      ffn_w_in, ffn_w_out)
    y_sample = _trunk(x_sample, norm_mix, norm_ffn, norm_final, rw_mu, rw_w_rkv, rw_w0, rw_w1, rw_w2,
                      rw_a0, rw_a1, rw_a2, rw_v0, rw_v1, rw_v2, rw_g1, rw_g2, rw_k_k, rw_k_a, rw_r_k,
                      rw_lnx_w, rw_lnx_b, rw_w_o, hg_w_in, hg_lb_logits, hg_gnorm, hg_w_o,
                      ffn_w_in, ffn_w_out)
    return (y_prompt, y_sample)
```

```python
import contextlib
import math
from collections import defaultdict

import numpy as np
import concourse.bass as bass
import concourse.mybir as mybir
from concourse.bass_utils import run_bass_kernel_spmd

F32 = mybir.dt.float32
BF16 = mybir.dt.bfloat16
AF = mybir.ActivationFunctionType
ALU = mybir.AluOpType
AX = mybir.AxisListType

SAME_ENGINE_SYNC = True
ATTACH_WAITS = True

D = 1024
KC = 8
FF = 2816
NCORES = 8
RW_LN_EPS = 64e-5
NORM_EPS = 1e-6
DECAY_C = -math.exp(-0.5)


class _Op:
    __slots__ = ("eng", "fn", "deps", "needed", "val", "semkey", "is_dma", "waits", "is_nop")


class Sched:
    ENG = ("pe", "act", "dve", "pool", "sp")

    def __init__(self, nc):
        self.nc = nc
        self.ops = []
        self.last_w = {}
        self.readers = {}
        self.stack = contextlib.ExitStack()
        self.last_eng = {}
        self.last_dma = {}

    def sbuf(self, name, shape, dtype):
        return self.stack.enter_context(self.nc.sbuf_tensor(name, list(shape), dtype))

    def psum(self, name, shape, dtype):
        return self.stack.enter_context(self.nc.psum_tensor(name, list(shape), dtype))

    def _add(self, eng, fn, reads, writes, is_dma, semkey, extra=None):
        deps = set()
        lw = self.last_w
        rd = self.readers
        for k in reads:
            w = lw.get(k)
            if w is not None:
                deps.add(w)
        for k in writes:
            w = lw.get(k)
            if w is not None:
                deps.add(w)
            r = rd.get(k)
            if r:
                deps.update(r)
        if extra:
            deps.update(extra)
        i = len(self.ops)
        op = _Op()
        op.eng = eng
        op.fn = fn
        op.deps = deps
        op.needed = is_dma
        op.val = 0
        op.semkey = semkey
        op.is_dma = is_dma
        op.waits = None
        op.is_nop = False
        self.ops.append(op)
        for k in reads:
            rd.setdefault(k, []).append(i)
        for k in writes:
            lw[k] = i
            rd[k] = []
        if is_dma:
            self.last_dma[semkey] = i
        else:
            self.last_eng[eng] = i
        return i

    def op(self, eng, fn, reads=(), writes=()):
        return self._add(eng, fn, reads, writes, False, eng)

    def dma(self, eng, out, in_, reads=(), writes=(), sem=None):
        return self._add(eng, lambda e: e.dma_start(out=out, in_=in_), reads, writes, True, "dma_" + sem)

    def barrier(self):
        extra = set(self.last_eng.values()) | set(self.last_dma.values())
        for eng in self.ENG:
            i = self._add(eng, lambda e: e.nop(), (), (), False, eng, extra=extra)
            self.ops[i].is_nop = True
        self.last_w = {}
        self.readers = {}

    def finalize(self):
        nc = self.nc
        ops = self.ops
        for op in ops:
            for d in op.deps:
                ops[d].needed = True
        count = defaultdict(int)
        known = {e: defaultdict(int) for e in self.ENG}
        per_eng = {e: [] for e in self.ENG}
        for op in ops:
            waits = {}
            kn = known[op.eng]
            for d in op.deps:
                dop = ops[d]
                if dop.is_dma:
                    sk = dop.semkey
                    val = count[sk] * 16
                else:
                    sk = dop.eng
                    val = dop.val
                    if dop.eng == op.eng and not op.is_dma:
                        if op.eng in ("pe", "sp") or not SAME_ENGINE_SYNC:
                            continue
                if kn[sk] >= val:
                    continue
                if waits.get(sk, 0) < val:
                    waits[sk] = val
            for sk, val in waits.items():
                kn[sk] = val
            op.waits = waits
            if op.needed:
                count[op.semkey] += 1
                op.val = count[op.semkey] * (16 if op.is_dma else 1)
            per_eng[op.eng].append(op)
        semkeys = sorted(count.keys())
        self.sem_final = {sk: count[sk] * (16 if sk.startswith("dma_") else 1) for sk in semkeys}
        sems = {}
        for sk in semkeys:
            sems[sk] = self.stack.enter_context(nc.semaphore("s_" + sk))
        self.n_sems = len(semkeys)
        final_waits = {sk: count[sk] * 16 for sk in semkeys if sk.startswith("dma_")}
        block = self.stack.enter_context(nc.Block())

        def emit(eng_name):
            lst = per_eng[eng_name]

            def body(e):
                attach = ATTACH_WAITS and eng_name in ("pe", "act", "dve")
                for op in lst:
                    wl = list(op.waits.items())
                    last = None
                    if attach and wl and not op.is_dma and not op.is_nop:
                        last = wl.pop()
                    for sk, val in wl:
                        e.wait_ge(sems[sk], val)
                    ins = op.fn(e)
                    if last is not None:
                        ins._wait_ge(sems[last[0]], last[1])
                    if op.needed:
                        ins.then_inc(sems[op.semkey], 16 if op.is_dma else 1)
                if eng_name == "pool":
                    for sk, val in final_waits.items():
                        e.wait_ge(sems[sk], val)

            return body

        block.tensor(emit("pe"))
        block.scalar(emit("act"))
        block.vector(emit("dve"))
        block.gpsimd(emit("pool"))
        block.sync(emit("sp"))
        self.counts = {e: len(per_eng[e]) for e in self.ENG}
        self.stack.close()


def _consts():
    idx = np.arange(128)
    s = idx[:, None]
    t = idx[None, :]
    blk = (s // 64) == (t // 64)
    c = {}
    c["ident"] = np.eye(128, dtype=np.float32)
    for d in (0, 1):
        strict = blk & ((s < t) if d == 0 else (s > t))
        incl = blk & ((s <= t) if d == 0 else (s >= t))
        after = blk & ((s > t) if d == 0 else (s < t))
        c["mask4_%d" % d] = np.concatenate([strict, incl, strict, incl], axis=1).astype(np.float32)
        c["maskn_%d" % d] = strict.T.astype(np.float32)
        c["maski_%d" % d] = incl.astype(np.float32)
        c["linc_%d" % d] = incl.astype(np.float32)
        c["lrem_%d" % d] = after.astype(np.float32)
    csel = np.zeros((128, 2), np.float32)
    csel[:64, 0] = 1.0
    csel[64:, 1] = 1.0
    c["csel"] = csel
    c["rcsel"] = (DECAY_C * csel).astype(np.float32)
    for d in (0, 1):
        for nm, mm_ in (("linc", (s <= t) if d == 0 else (s >= t)), ("lexc", (s < t) if d == 0 else (s > t)), ("lrem", (s > t) if d == 0 else (s < t))):
            c["r%s_%d" % (nm, d)] = (DECAY_C * (blk & mm_)).astype(np.float32)
    c["ones_row"] = np.ones((1, 128), np.float32)
    sel = np.zeros((5, 5 * 128), np.float32)
    for i in range(5):
        sel[i, i * 128:(i + 1) * 128] = 1.0
    c["sel5"] = sel
    return c


CONST_SHAPES = {k: v.shape for k, v in _consts().items()}

WEIGHT_SHAPES = {
    "norm_mix": (4, D), "norm_ffn": (4, D), "norm_final": (D,), "rw_mu": (2, 6, D),
    "rw_w_rkv": (2, 3, D, D), "rw_w0": (2, 2, D), "rw_w1": (2, 2, D, 64), "rw_w2": (2, 2, 64, D),
    "rw_a0": (2, 2, D), "rw_a1": (2, 2, D, 64), "rw_a2": (2, 2, 64, D), "rw_v0": (1, D),
    "rw_v1": (1, D, 32), "rw_v2": (1, 32, D), "rw_g1": (2, D, 160), "rw_g2": (2, 160, D),
    "rw_k_k": (2, D), "rw_k_a": (2, D), "rw_r_k": (2, D), "rw_lnx_w": (2, D), "rw_lnx_b": (2, D),
    "rw_w_o": (2, D, D), "hg_w_in": (2, D, 5 * D), "hg_lb_logits": (4, D), "hg_gnorm": (2, D),
    "hg_w_o": (2, D, D), "ffn_w_in": (4, D, 2 * FF), "ffn_w_out": (4, FF, D),
}


class _Cut(Exception):
    pass


class Builder:
    def cutpoint(self, n):
        if getattr(self, "cut", 0) == n:
            raise _Cut()

    def __init__(self, seq_lens, layers, final_norm=True):
        self.seq_lens = list(seq_lens)
        self.layers = list(layers)
        self.final_norm = final_norm
        self.T = sum(seq_lens)
        self.NT = self.T // 128
        nc = bass.Bass("TRN2", target_bir_lowering=False)
        self.nc = nc
        self.S = Sched(nc)
        S = self.S
        T = self.T
        self.x_in = nc.dram_tensor("x", [T, D], F32, kind="ExternalInput").ap()
        self.y_out = nc.dram_tensor("y", [T, D], F32, kind="ExternalOutput").ap()
        self.w = {k: nc.dram_tensor(k, list(s), F32, kind="ExternalInput").ap() for k, s in WEIGHT_SHAPES.items()}
        self.cin = {k: nc.dram_tensor("c_" + k, list(s), F32, kind="ExternalInput").ap() for k, s in CONST_SHAPES.items()}
        self.XS = nc.dram_tensor("XS", [T, D], F32).ap()
        self.XS2 = nc.dram_tensor("XS2", [T, D], F32).ap()
        nseq = len(self.seq_lens)
        self.HIN = nc.dram_tensor("HIN", [T + 2 * nseq, D], BF16).ap()
        self.OACC = nc.dram_tensor("OACC", [T, D], F32).ap()
        self.GATE = nc.dram_tensor("GATE", [T, D], F32).ap()
        self.BONUS = nc.dram_tensor("BONUS", [T, D], F32).ap()
        self.VFIRST = nc.dram_tensor("VFIRST", [T, D], F32).ap()
        self.QT = nc.dram_tensor("QT", [self.NT, 128, D], BF16).ap()
        self.KH = nc.dram_tensor("KH", [T, D], BF16).ap()
        self.VV = nc.dram_tensor("VV", [T, D], BF16).ap()
        self.ETOT = nc.dram_tensor("ETOT", [self.NT, 128, 16], F32).ap()
        self.PT = nc.dram_tensor("PT", [self.NT, 64, 16 * 128], BF16).ap()
        self.GG = nc.dram_tensor("GG", [self.NT, 64, 16 * 128], BF16).ap()
        self.RT = nc.dram_tensor("RT", [self.NT, 64, 16 * 128], BF16).ap()
        self.identb = S.sbuf("identb", [128, 128], BF16)
        self.identf = S.sbuf("identf", [128, 128], F32)
        self.cs = {}
        for k in CONST_SHAPES:
            if k in ("ident", "sel5", "ones_row"):
                continue
            shp = CONST_SHAPES[k]
            self.cs[k] = S.sbuf("cs_" + k, list(shp), BF16 if k.startswith("mask") else F32)
        self.sel5 = S.sbuf("sel5", [5, 640], BF16)
        self.A16_N = 68400
        self.A32_N = 15660
        self.A16 = S.sbuf("A16", [128, self.A16_N], BF16)
        self.A32 = S.sbuf("A32", [128, self.A32_N], F32)
        self.PS = S.psum("PS", [128, 6 * 512], F32)
        self.PT16 = S.psum("PT16", [128, 2 * 1024], BF16)
        self.p16 = 0
        self.p32 = 0
        self.phase_id = 0
        self.dbg = {}
        self.sem_map = {}
        self.sem_next = {}
        self.seq_tiles = []
        r = 0
        for si, L in enumerate(self.seq_lens):
            self.seq_tiles.append([(r // 128 + i) for i in range(L // 128)])
            r += L

    def reset(self):
        self.S.barrier()
        self.p16 = 0
        self.p32 = 0
        self.phase_id += 1

    def a16(self, n, name):
        assert self.p16 + n <= self.A16_N, ("A16 overflow", name, self.p16, n)
        ap = self.A16[:, self.p16:self.p16 + n]
        self.p16 += n
        return ap, "%s@%d" % (name, self.phase_id)

    def a32(self, n, name):
        assert self.p32 + n <= self.A32_N, ("A32 overflow", name, self.p32, n)
        ap = self.A32[:, self.p32:self.p32 + n]
        self.p32 += n
        return ap, "%s@%d" % (name, self.phase_id)

    def bank(self, i):
        return self.PS[:, i * 512:(i + 1) * 512], "psb%d" % i

    def tbank(self, i):
        return self.PT16[:, i * 1024:(i + 1) * 1024], "ptb%d" % i

    def mm(self, out, lhsT, rhs, start, stop, reads, writes):
        self.S.op("pe", lambda e: e.matmul(out, lhsT, rhs, start=start, stop=stop), reads, writes)

    def tr(self, out, in_, reads, writes):
        idt = self.identb[0:in_.shape[0], 0:in_.shape[0]]
        self.S.op("pe", lambda e: e.transpose(out, in_, idt), list(reads) + ["identb"], writes)

    def act(self, out, in_, func, reads, writes, **kw):
        self.S.op("act", lambda e: e.activation(out=out, in_=in_, func=func, **kw), reads, writes)

    def tt(self, out, in0, in1, op, reads, writes, eng="dve"):
        self.S.op(eng, lambda e: e.tensor_tensor(out=out, in0=in0, in1=in1, op=op), reads, writes)

    def stt(self, out, in0, scalar, in1, op0, op1, reads, writes, eng="dve"):
        self.S.op(eng, lambda e: e.scalar_tensor_tensor(out=out, in0=in0, scalar=scalar, in1=in1, op0=op0, op1=op1), reads, writes)

    def ts(self, out, in0, s1, s2, op0, op1, reads, writes, eng="dve"):
        if s2 is None:
            self.S.op(eng, lambda e: e.tensor_scalar(out=out, in0=in0, scalar1=s1, scalar2=None, op0=op0), reads, writes)
        else:
            self.S.op(eng, lambda e: e.tensor_scalar(out=out, in0=in0, scalar1=s1, scalar2=s2, op0=op0, op1=op1), reads, writes)

    def reduce(self, out, in_, reads, writes):
        self.S.op("dve", lambda e: e.tensor_reduce(out=out, in_=in_, axis=AX.X, op=ALU.add), reads, writes)

    def recip(self, out, in_, reads, writes):
        self.S.op("dve", lambda e: e.reciprocal(out=out, in_=in_), reads, writes)

    def memset(self, ap, val, writes):
        self.S.op("dve", lambda e: e.memset(ap, val), [], writes)

    def copy(self, out, in_, reads, writes, eng="dve"):
        if eng == "act":
            self.act(out, in_, AF.Copy, reads, writes)
        else:
            self.S.op(eng, lambda e: e.tensor_copy(out=out, in_=in_), reads, writes)

    def _semname(self, sem):
        key = (self.phase_id, sem)
        m = self.sem_map
        if key not in m:
            n = self.sem_next.get(self.phase_id, 0)
            self.sem_next[self.phase_id] = n + 1
            m[key] = "q%d" % n
        return m[key]

    def load(self, out, in_, writes, sem, reads=(), eng="sp"):
        self.S.dma(eng, out, in_, reads=reads, writes=writes, sem=self._semname(sem))

    def store(self, out, in_, reads, sem, writes=(), eng="pool"):
        self.S.dma(eng, out, in_, reads=reads, writes=writes, sem=self._semname(sem))

    def dump(self, name, ap, key, shape, dtype=F32):
        if not getattr(self, "debug", False):
            return
        if name in self.dbg:
            return
        t = self.nc.dram_tensor("dbg_" + name, list(shape), dtype, kind="ExternalOutput").ap()
        self.dbg[name] = t
        self.store(t, ap, [key] if isinstance(key, str) else key, "dbg_" + name)

    def load_consts(self):
        S = self.S
        self.load(self.identb[:], self.cin["ident"][:, :], ["identb"], "identb", eng="pool")
        self.load(self.identf[:], self.cin["ident"][:, :], ["identf"], "identf")
        for k, t in self.cs.items():
            self.load(t[:], self.cin[k][:, :], ["cs_" + k], "cs_" + k, eng=("pool" if k.startswith("mask") else "sp"))
        self.load(self.sel5[:], self.cin["sel5"][:, :], ["sel5"], "sel5", eng="pool")

    def wload(self, name, n, src_ap, shape_str=None, **kw):
        ap, key = self.a16(n, name)
        dst = ap if shape_str is None else ap.rearrange(shape_str, **kw)
        self.load(dst, src_ap, [key], "w_" + name, eng="pool")
        return ap, key

    def bcload(self, name, src_row):
        ap, key = self.a32(D, name)
        self.load(ap, src_row.partition_broadcast(128), [key], "bc_" + name)
        return ap, key

    def rmsnorm_tile(self, x_ap, xk, wbc, wk, out_ap, outk, junk, junkk, st, stk):
        self.act(junk, x_ap, AF.Square, [xk], [junkk])
        self.reduce(st[:, 0:1], junk, [junkk], [stk])
        self.ts(st[:, 1:2], st[:, 0:1], 1.0 / D, NORM_EPS, ALU.mult, ALU.add, [stk], [stk])
        self.act(st[:, 2:3], st[:, 1:2], AF.Sqrt, [stk], [stk])
        self.recip(st[:, 3:4], st[:, 2:3], [stk], [stk])
        self.stt(out_ap, x_ap, st[:, 3:4], wbc, ALU.mult, ALU.mult, [xk, stk, wk], [outk])

    def transpose8(self, src, srck, dst, dstk, tb, evac_eng="act"):
        pb, pk = self.tbank(tb)
        for k in range(KC):
            self.tr(pb[:, k * 128:(k + 1) * 128], src[:, k * 128:(k + 1) * 128], [srck], [pk])
        self.copy(dst, pb, [pk], [dstk], eng=evac_eng)

    def ffn_phase(self, li, half, src, dst):
        self.reset()
        HB = 11
        h0 = half * HB * 128
        win, wink = self.a16(KC * 2 * HB * 128, "win")
        winv = win.rearrange("p (k c) -> p k c", k=KC)
        wsrc = self.w["ffn_w_in"][li].rearrange("(k p) c -> p k c", p=128)
        for k in range(KC):
            self.load(winv[:, k, 0:HB * 128], wsrc[:, k, h0:h0 + HB * 128], [wink + "g"], "w_wing", eng="pool")
            self.load(winv[:, k, HB * 128:2 * HB * 128], wsrc[:, k, FF + h0:FF + h0 + HB * 128], [wink + "u"], "w_winu", eng="pool")
        wout, woutk = self.a16(HB * D, "wout")
        woutv = wout.rearrange("p (j c) -> p j c", j=HB)
        for j in range(HB):
            self.load(woutv[:, j, :], self.w["ffn_w_out"][li][h0 + j * 128:h0 + (j + 1) * 128, :], [woutk], "w_wout", eng="pool")
        nw, nwk = self.bcload("nffn", self.w["norm_ffn"][li])
        NB = 2
        xt = [self.a32(4 * D, "xt%d" % i) for i in range(NB)]
        st, stk = self.a32(4, "st")
        junk, junkk = self.a32(D, "junk")
        xn, xnk = self.a16(D, "xn")
        xnT = [self.a16(KC * 512, "xnT%d" % i) for i in range(NB)]
        hT, hTk = self.a16(HB * 512, "hT")
        hTv = hT.rearrange("p (j t) -> p j t", j=HB)
        sg = [self.a32(512, "sg%d" % i) for i in range(2)]
        ot = [self.a32(D, "ot%d" % i) for i in range(2)]
        nsup = (self.NT + 3) // 4
        for su in range(nsup):
            tiles = list(range(su * 4, min(self.NT, su * 4 + 4)))
            nt = len(tiles)
            ntok = nt * 128
            xa, xak = xt[su % NB]
            xav = xa.rearrange("p (i c) -> p i c", i=4)
            xTa, xTk = xnT[su % NB]
            xTv = xTa.rearrange("p (k t) -> p k t", k=KC)
            r0 = tiles[0] * 128
            self.load(xav[:, 0:nt, :], self.XS[r0:r0 + ntok, :].rearrange("(i p) c -> p i c", p=128), [xak], "xt%d" % (su % NB))
            for i in range(nt):
                self.rmsnorm_tile(xav[:, i, :], xak, nw, nwk, xn, xnk, junk, junkk, st, stk)
                pb, pk = self.tbank(i % 2)
                for k in range(KC):
                    self.tr(pb[:, k * 128:(k + 1) * 128], xn[:, k * 128:(k + 1) * 128], [xnk], [pk])
                self.copy(xTv[:, :, i * 128:(i + 1) * 128], pb.rearrange("p (k t) -> p k t", k=KC), [pk], [xTk], eng="act")
            for j in range(HB):
                gb, gk = self.bank((2 * j) % 4)
                ub, uk = self.bank((2 * j + 1) % 4)
                for k in range(KC):
                    self.mm(gb[:, 0:ntok], winv[:, k, j * 128:(j + 1) * 128], xTv[:, k, 0:ntok], k == 0, k == KC - 1, [wink + "g", xTk], [gk])
                for k in range(KC):
                    self.mm(ub[:, 0:ntok], winv[:, k, (HB + j) * 128:(HB + j + 1) * 128], xTv[:, k, 0:ntok], k == 0, k == KC - 1, [wink + "u", xTk], [uk])
                sga, sgk = sg[j % 2]
                self.act(sga[:, 0:ntok], gb[:, 0:ntok], AF.Silu, [gk], [sgk])
                self.tt(hTv[:, j, 0:ntok], sga[:, 0:ntok], ub[:, 0:ntok], ALU.mult, [sgk, uk], [hTk])
            if src is not self.XS:
                self.load(xav[:, 0:nt, :], src[r0:r0 + ntok, :].rearrange("(i p) c -> p i c", p=128), [xak], "xt%d" % (su % NB))
            for i in range(nt):
                oa, oak = ot[i % 2]
                for cg in range(2):
                    ob, obk = self.bank(4 + cg)
                    for j in range(HB):
                        self.mm(ob, hTv[:, j, i * 128:(i + 1) * 128], woutv[:, j, cg * 512:(cg + 1) * 512], j == 0, j == HB - 1, [hTk, woutk], [obk])
                    self.tt(oa[:, cg * 512:(cg + 1) * 512], xav[:, i, cg * 512:(cg + 1) * 512], ob, ALU.add, [xak, obk], [oak])
                g = tiles[i]
                self.store(dst[g * 128:(g + 1) * 128, :], oa, [oak], "ot%d" % (i % 2))

    def ffn_layer(self, li):
        self.ffn_phase(li, 0, self.XS, self.XS2)
        self.ffn_phase(li, 1, self.XS2, self.XS3)
        self.XS, self.XS3 = self.XS3, self.XS

    def rwkv_layer(self, li, j):
        S = self.S
        nseq = len(self.seq_lens)
        self.reset()
        nw, nwk = self.bcload("nmix", self.w["norm_mix"][li])
        xt = [self.a32(D, "x%d" % i) for i in range(2)]
        hn = [self.a16(D, "hn%d" % i) for i in range(2)]
        junk, junkk = self.a32(D, "junk")
        st, stk = self.a32(4, "st")
        z, zk = self.a16(D, "z")
        self.memset(z[0:1, :], 0.0, [zk])
        row = 0
        hbase = []
        for si, tiles in enumerate(self.seq_tiles):
            L = len(tiles) * 128
            hb0 = row + 2 * si + 1
            hbase.append(hb0)
            self.store(self.HIN[hb0 - 1:hb0, :], z[0:1, :], [zk], "r0z")
            self.store(self.HIN[hb0 + L:hb0 + L + 1, :], z[0:1, :], [zk], "r0z")
            for ti, g in enumerate(tiles):
                xa, xak = xt[g % 2]
                ha, hak = hn[g % 2]
                self.load(xa, self.XS[g * 128:(g + 1) * 128, :], [xak], "r0x%d" % (g % 2))
                self.rmsnorm_tile(xa, xak, nw, nwk, ha, hak, junk, junkk, st, stk)
                self.store(self.HIN[hb0 + ti * 128:hb0 + (ti + 1) * 128, :], ha, [hak], "r0h%d" % (g % 2))
            row += L
        if getattr(self, "stop_after", "") == "R0":
            return
        self.reset()
        wr, wrk = self.a16(3 * KC * D, "wrkv")
        wrv = wr.rearrange("p (m k c) -> p m k c", m=3, k=KC)
        for m in range(3):
            for k in range(KC):
                self.load(wrv[:, m, k, :], self.w["rw_w_rkv"][j][m][k * 128:(k + 1) * 128, :], [wrk], "w_rkv", eng="pool")
        g1, g1k = self.a16(KC * 160, "g1")
        g1v = g1.rearrange("p (k c) -> p k c", k=KC)
        for k in range(KC):
            self.load(g1v[:, k, :], self.w["rw_g1"][j][k * 128:(k + 1) * 128, :], [g1k], "w_g1", eng="pool")
        g2, g2k = self.a16(2 * D, "g2")
        g2v = g2.rearrange("p (a c) -> p a c", a=2)
        self.load(g2v[:, 0, :], self.w["rw_g2"][j][0:128, :], [g2k], "w_g2", eng="pool")
        self.load(g2v[0:32, 1, :], self.w["rw_g2"][j][128:160, :], [g2k], "w_g2", eng="pool")
        w1 = []
        a1 = []
        w2 = []
        a2 = []
        brow, browk = self.a16(D, "brow")
        self.memset(brow[0:5, :], 0.0, [browk])
        for d in range(2):
            t_, k_ = self.a16(KC * 64, "w1_%d" % d)
            for k in range(KC):
                self.load(t_[:, k * 64:(k + 1) * 64], self.w["rw_w1"][j][d][k * 128:(k + 1) * 128, :], [k_], "w_w1", eng="pool")
            w1.append((t_.rearrange("p (k c) -> p k c", k=KC), k_))
            t_, k_ = self.a16(KC * 64, "a1_%d" % d)
            for k in range(KC):
                self.load(t_[:, k * 64:(k + 1) * 64], self.w["rw_a1"][j][d][k * 128:(k + 1) * 128, :], [k_], "w_a1", eng="pool")
            a1.append((t_.rearrange("p (k c) -> p k c", k=KC), k_))
            t_, k_ = self.a16(D, "w2_%d" % d)
            self.load(t_[0:64, :], self.w["rw_w2"][j][d], [k_], "w_w2", eng="pool")
            w2.append((t_, k_))
            t_, k_ = self.a16(D, "a2_%d" % d)
            self.load(t_[0:64, :], self.w["rw_a2"][j][d], [k_], "w_a2", eng="pool")
            a2.append((t_, k_))
            self.load(brow[d:d + 1, :], self.w["rw_w0"][j][d].partition_broadcast(1), [browk], "w_b0", eng="pool")
            self.load(brow[2 + d:3 + d, :], self.w["rw_a0"][j][d].partition_broadcast(1), [browk], "w_b0", eng="pool")
        if j > 0:
            v1, v1k = self.a16(KC * 32, "v1")
            v1v = v1.rearrange("p (k c) -> p k c", k=KC)
            for k in range(KC):
                self.load(v1v[:, k, :], self.w["rw_v1"][j - 1][k * 128:(k + 1) * 128, :], [v1k], "w_v1", eng="pool")
            v2, v2k = self.a16(D, "v2")
            self.load(v2[0:32, 0:D], self.w["rw_v2"][j - 1], [v2k], "w_v2", eng="pool")
            self.load(brow[4:5, :], self.w["rw_v0"][j - 1].partition_broadcast(1), [browk], "w_b0", eng="pool")
        mu, muk = self.a32(64, "mu")
        mu48, mu48k = E1x = self.a32(128, "mu48")
        self.load(mu48[0:48, :], self.w["rw_mu"][j].rearrange("i (k p) -> (i k) p", p=128), [mu48k], "mu48")
        mb, mbk = self.bank(0)
        self.mm(mb[:, 0:48], mu48[0:48, :], self.identf[0:48, 0:48], True, True, [mu48k, "identf"], [mbk])
        self.copy(mu[:, 0:48], mb[:, 0:48], [mbk], [muk])
        kkb, kkbk = self.bcload("k_k", self.w["rw_k_k"][j])
        kab, kabk = self.bcload("k_a", self.w["rw_k_a"][j])
        rkb, rkbk = self.bcload("r_k", self.w["rw_r_k"][j])
        r32, r32k = self.a32(D, "r32")
        k32, k32k = self.a32(D, "k32")
        v32, v32k = self.a32(D, "v32")
        kk, kkk = self.a32(D, "kk")
        sgw, sgwk = self.a32(D, "sgw")
        asg, asgk = self.a32(D, "asg")
        kd, kdk = self.a32(D, "kd")
        b32, b32k = self.a32(D, "b32")
        E = [self.a32(D, "E%d" % i) for i in range(2)]
        bon, bonk = self.a32(D, "bon")
        oacc, oacck = self.a32(D, "oacc")
        st, stk = self.a32(64, "st")
        etot, etk = self.a32(32, "etot")
        Gst, Gstk = self.a16(16 * 128, "Gst")
        Gv = Gst[0:64, :].rearrange("p (h q v) -> p h q v", h=16, q=2)
        cur, curk = self.a16(D, "cur")
        sh, shk = self.a16(D, "sh")
        curT, curTk = self.a16(D, "curT")
        shT, shTk = self.a16(D, "shT")
        curTv = curT.rearrange("p (k t) -> p k t", k=KC)
        shTv = shT.rearrange("p (k t) -> p k t", k=KC)
        mixbuf = [self.a16(D, "mix%d" % i) for i in range(6)]
        mix = [(m[0][:, 0:D], m[1]) for m in mixbuf]
        mixv = [m[0].rearrange("p (k t) -> p k t", k=KC) for m in mix]
        sm, smk = self.a16(2 * 128, "smallT")
        V, Vk = mix[3]
        rt_, rtk = cur, curk
        bt_, btk = sh, shk
        kt_, ktk = curT, curTk
        at_, atk = shT, shTk
        Bh, Bhk = mix[0]
        Kh, Khk = mix[2]
        BhX = mixbuf[0][0]
        KhX = mixbuf[2][0]
        XX = mixbuf[5][0]
        arT, arTk = self.a16(16 * 256, "arT")
        arTv = arT[0:64, :].rearrange("p (h a t) -> p h a t", h=16, a=2)
        bT, bTk = self.a16(16 * 128, "bT")
        bTv = bT[0:64, :].rearrange("p (h t) -> p h t", h=16)
        kT, kTk = self.a16(16 * 128, "kT")
        kTv = kT[0:64, :].rearrange("p (h t) -> p h t", h=16)
        SC, SCk = self.a16(4 * 512, "SC")
        SCv = SC.rearrange("p (h c) -> p h c", h=4)
        Mp = [None] + [self.a16(512, "Mp%d" % i) for i in range(1, 6)]
        NX, NXk = self.a16(1536, "NX")
        Np2 = [(NX[:, 0:512], NXk), (NX[:, 512:1024], NXk)]
        Np = [Np2[i % 2] for i in range(6)]
        X2v = NX.rearrange("p (h q c) -> p h q c", h=4, q=2)
        X = [(mix[5][0][:, 0:512], mix[5][1]), (mix[5][0][:, 512:1024], mix[5][1])]
        PTs, PTsk = self.a16(16 * 128, "PTs")
        PTv = PTs[0:64, :].rearrange("p (h q k) -> p h q k", h=16, q=2)
        RTs, RTsk = self.a16(16 * 128, "RTs")
        RTv = RTs[0:64, :].rearrange("p (h t) -> p h t", h=16)
        Hbf, Hbfk = self.a16(D, "Hbf")
        Hv = Hbf[0:64, :].rearrange("p (h v) -> p h v", h=16)
        rot = [0]

        def nb():
            rot[0] = (rot[0] + 1) % 4
            return self.bank(rot[0])

        def lin2(lhs_chunks, rhs_fn, evac):
            for cg in range(2):
                pb, pk = nb()
                n = len(lhs_chunks)
                for i, (l, lk) in enumerate(lhs_chunks):
                    r, rk_ = rhs_fn(i, cg)
                    self.mm(pb, l, r, i == 0, i == n - 1, [lk, rk_], [pk])
                evac(cg, pb, pk)

        def recur(d, q_order, hg, oa, oak):
            ib, ibk = self.bank(5)
            for qi, q in enumerate(q_order):
                hb_, hbk_ = nb()
                for hh in range(4):
                    h = hg * 4 + hh
                    self.mm(ib[:, q * 256 + hh * 64:q * 256 + (hh + 1) * 64], RTv[:, h, :], Hv[:, h, :], True, True, [RTsk, Hbfk + str(hg)], [ibk])
                    self.mm(hb_[0:64, hh * 64:(hh + 1) * 64], PTv[:, h, q, :], Hv[:, h, :], True, True, [PTsk, Hbfk + str(hg)], [hbk_])
                self.tt(Hv[:, hg * 4:(hg + 1) * 4, :], hb_[0:64, 0:256].rearrange("p (h v) -> p h v", h=4), Gv[:, hg * 4:(hg + 1) * 4, q, :], ALU.add,
                        [hbk_, Gstk], [Hbfk + str(hg)])
            cs = slice(hg * 256, (hg + 1) * 256)
            for q in range(2):
                ps = slice(64 * q, 64 * q + 64)
                self.tt(oa[ps, cs], oa[ps, cs], ib[ps, q * 256:(q + 1) * 256], ALU.add, [ibk, oak], [oak])

        def zero_H():
            self.memset(Hbf[0:64, :], 0.0, [Hbfk + str(hg) for hg in range(4)])

        for si, tiles in enumerate(self.seq_tiles):
            zero_H()
            for ti, g in enumerate(tiles):
                try:
                    r0 = g * 128
                    hb = hbase[si] + ti * 128
                    self.load(cur, self.HIN[hb:hb + 128, :], [curk], "r1cur")
                    self.load(sh[:, 0:512], self.HIN[hb - 1:hb + 127, 0:512], [shk], "r1sh")
                    self.load(sh[:, 512:D], self.HIN[hb + 1:hb + 129, 512:D], [shk], "r1sh")
                    self.transpose8(cur, curk, curT, curTk, 0)
                    self.transpose8(sh, shk, shT, shTk, 1)
                    self.tt(shT, shT, curT, ALU.subtract, [shTk, curTk], [shTk])
                    for i in range(6):
                        for k in range(KC):
                            self.stt(mixv[i][:, k, :], shTv[:, k, :], mu[:, i * 8 + k:i * 8 + k + 1], curTv[:, k, :], ALU.mult, ALU.add,
                                     [shTk, muk, curTk], [mix[i][1]])
                    self.cutpoint(1)
                    for m, (mi, dst, dk) in ((1, (2, k32, k32k)), (2, (3, v32, v32k)), (0, (0, r32, r32k))):
                        lin2([(mixv[mi][:, k, :], mix[mi][1]) for k in range(KC)], lambda i, cg, m=m: (wrv[:, m, i, cg * 512:(cg + 1) * 512], wrk),
                             lambda cg, pb, pk, dst=dst, dk=dk, m=m: self.copy(dst[:, cg * 512:(cg + 1) * 512], pb, [pk], [dk], eng=("act" if m != 1 else "dve")))
                    if g == 0:
                        self.dump("r32a", r32, r32k, [128, D])
                    if j == 0:
                        self.store(self.VFIRST[r0:r0 + 128, :], v32, [v32k], "r1vf")
                    else:
                        vb_, vbk_ = nb()
                        for k in range(KC):
                            self.mm(vb_[0:32, 0:128], v1v[:, k, :], mixv[3][:, k, :], k == 0, k == KC - 1, [v1k, mix[3][1]], [vbk_])
                        self.copy(sm[0:32, 0:128], vb_[0:32, 0:128], [vbk_], [smk])
                        self.load(E[0][0], self.VFIRST[r0:r0 + 128, :], [E[0][1]], "r1vfl")
                        lin2([(sm[0:32, 0:128], smk), (self.sel5[0:5, 512:640], "sel5")],
                             lambda i, cg: ((v2[0:32, cg * 512:(cg + 1) * 512], v2k) if i == 0 else (brow[0:5, cg * 512:(cg + 1) * 512], browk)),
                             lambda cg, pb, pk: self.act(E[1][0][:, cg * 512:(cg + 1) * 512], pb, AF.Sigmoid, [pk], [E[1][1]]))
                        self.tt(E[0][0], E[0][0], v32, ALU.subtract, [E[0][1], v32k], [E[0][1]])
                        self.tt(E[0][0], E[0][0], E[1][0], ALU.mult, [E[0][1], E[1][1]], [E[0][1]])
                        self.tt(v32, v32, E[0][0], ALU.add, [v32k, E[0][1]], [v32k])
                    self.copy(V, v32, [v32k], [Vk], eng="act")
                    if g == 0:
                        self.dump("r32b", r32, r32k, [128, D])
                    for mg, (m0, m1) in enumerate(((0, 128), (128, 160))):
                        gb_, gbk_ = nb()
                        for k in range(KC):
                            self.mm(gb_[0:m1 - m0, 0:128], g1v[:, k, m0:m1], mixv[5][:, k, :], k == 0, k == KC - 1, [g1k, mix[5][1]], [gbk_])
                        self.act(sm[0:m1 - m0, mg * 128:(mg + 1) * 128], gb_[0:m1 - m0, 0:128], AF.Sigmoid, [gbk_], [smk + "g"])
                    lin2([(sm[:, 0:128], smk + "g"), (sm[0:32, 128:256], smk + "g")],
                         lambda i, cg: (g2v[:, 0, cg * 512:(cg + 1) * 512], g2k) if i == 0 else (g2v[0:32, 1, cg * 512:(cg + 1) * 512], g2k),
                         lambda cg, pb, pk: self.copy(E[1][0][:, cg * 512:(cg + 1) * 512], pb, [pk], [E[1][1]], eng="act"))
                    self.store(self.GATE[r0:r0 + 128, :], E[1][0], [E[1][1]], "r1gate")
                    self.cutpoint(2)
                    if g == 0:
                        self.dump("r32", r32, r32k, [128, D])
                        self.dump("k32", k32, k32k, [128, D])
                        self.dump("v32", v32, v32k, [128, D])
                        self.dump("gate", E[1][0], E[1][1], [128, D])
                        self.dump("mix1", mix[1][0], mix[1][1], [128, D], BF16)
                        self.dump("mix0", mix[0][0], mix[0][1], [128, D], BF16)
                        self.dump("mu", mu[:, 0:48], muk, [128, 48])
                        self.dump("wr0", wr[:, 0:D], wrk, [128, D], BF16)
                        self.dump("wrall", wr[:, 0:8 * D], wrk, [128, 8 * D], BF16)
                        self.dump("wk0", wr[:, 8 * D:9 * D], wrk, [128, D], BF16)
                        self.dump("curT", curT, curTk, [128, D], BF16)
                        self.dump("kkb", kkb, kkbk, [128, D])
                    self.tt(kk, k32, kkb, ALU.mult, [k32k, kkbk], [kkk])
                    self.act(E[0][0], kk, AF.Square, [kkk], [E[0][1]])
                    self.reduce(st[:, 0:16], E[0][0].rearrange("p (h n) -> p h n", h=16), [E[0][1]], [stk])
                    self.act(st[:, 16:32], st[:, 0:16], AF.Sqrt, [stk], [stk])
                    self.ts(st[:, 16:32], st[:, 16:32], 1e-12, None, ALU.max, None, [stk], [stk])
                    self.recip(st[:, 32:48], st[:, 16:32], [stk], [stk])
                    self.tt(kk.rearrange("p (h n) -> p h n", h=16), kk.rearrange("p (h n) -> p h n", h=16),
                            st[:, 32:48].unsqueeze(2).to_broadcast([128, 16, 64]), ALU.mult, [kkk, stk], [kkk])
                    self.cutpoint(3)
                    for d in range(2):
                        wb_, wbk_ = nb()
                        for k in range(KC):
                            self.mm(wb_[0:64, 0:128], w1[d][0][:, k, :], mixv[1][:, k, :], k == 0, k == KC - 1, [w1[d][1], mix[1][1]], [wbk_])
                        self.act(sm[0:64, 0:128], wb_[0:64, 0:128], AF.Tanh, [wbk_], [smk])
                        lin2([(sm[0:64, 0:128], smk), (self.sel5[0:5, d * 128:(d + 1) * 128], "sel5")],
                             lambda i, cg, d=d: ((w2[d][0][0:64, cg * 512:(cg + 1) * 512], w2[d][1]) if i == 0 else (brow[0:5, cg * 512:(cg + 1) * 512], browk)),
                             lambda cg, pb, pk: self.act(sgw[:, cg * 512:(cg + 1) * 512], pb, AF.Sigmoid, [pk], [sgwk]))
                        ab_, abk_ = nb()
                        for k in range(KC):
                            self.mm(ab_[0:64, 0:128], a1[d][0][:, k, :], mixv[4][:, k, :], k == 0, k == KC - 1, [a1[d][1], mix[4][1]], [abk_])
                        self.copy(sm[0:64, 128:256], ab_[0:64, 0:128], [abk_], [smk + "a"], eng="act")
                        lin2([(sm[0:64, 128:256], smk + "a"), (self.sel5[0:5, (2 + d) * 128:(3 + d) * 128], "sel5")],
                             lambda i, cg, d=d: ((a2[d][0][0:64, cg * 512:(cg + 1) * 512], a2[d][1]) if i == 0 else (brow[0:5, cg * 512:(cg + 1) * 512], browk)),
                             lambda cg, pb, pk: self.act(asg[:, cg * 512:(cg + 1) * 512], pb, AF.Sigmoid, [pk], [asgk]))
                        self.stt(kd, asg, -1.0, kab, ALU.add, ALU.mult, [asgk, kabk], [kdk])
                        self.stt(kd, kd, 1.0, k32, ALU.add, ALU.mult, [kdk, k32k], [kdk])
                        self.tt(b32, kk, asg, ALU.mult, [kkk, asgk], [b32k])
                        self.tt(E[0][0], r32, rkb, ALU.mult, [r32k, rkbk], [E[0][1]])
                        self.tt(E[0][0], E[0][0], kd, ALU.mult, [E[0][1], kdk], [E[0][1]])
                        self.reduce(st[:, 48:64], E[0][0].rearrange("p (h n) -> p h n", h=16), [E[0][1]], [stk])
                        if d == 0:
                            self.tt(bon.rearrange("p (h n) -> p h n", h=16), v32.rearrange("p (h n) -> p h n", h=16),
                                    st[:, 48:64].unsqueeze(2).to_broadcast([128, 16, 64]), ALU.mult, [v32k, stk], [bonk])
                        else:
                            self.tt(E[0][0].rearrange("p (h n) -> p h n", h=16), v32.rearrange("p (h n) -> p h n", h=16),
                                    st[:, 48:64].unsqueeze(2).to_broadcast([128, 16, 64]), ALU.mult, [v32k, stk], [E[0][1]])
                            self.tt(bon, bon, E[0][0], ALU.add, [bonk, E[0][1]], [bonk])
                        tb, tbk = self.bank(4)
                        for h in range(16):
                            self.mm(tb[0:64, h * 2:h * 2 + 2], sgw[:, h * 64:(h + 1) * 64], self.cs["rcsel"][:], True, True, [sgwk, "cs_rcsel"], [tbk])
                        self.act(etot[0:64, :], tb[0:64, 0:32], AF.Exp, [tbk], [etk])
                        self.cutpoint(4)
                        if g == 0 and d == 0:
                            self.dump("kk", kk, kkk, [128, D])
                            self.dump("st", st, stk, [128, 64])
                            self.dump("sgw", sgw, sgwk, [128, D])
                            self.dump("asg", asg, asgk, [128, D])
                            self.dump("kd", kd, kdk, [128, D])
                            self.dump("etot", etot[0:64, :], etk, [64, 32])
                        def cums(nm):
                            bs = [nb(), nb()]
                            for c in range(2):
                                self.mm(bs[c][0], self.cs["r%s_%d" % (nm, d)][:], sgw[:, c * 512:(c + 1) * 512], True, True, ["cs_r%s_%d" % (nm, d), sgwk], [bs[c][1]])
                            return bs
                        gbs = cums("linc")
                        for c in range(2):
                            cs = slice(c * 512, (c + 1) * 512)
                            self.act(E[0][0][:, cs], gbs[c][0], AF.Exp, [gbs[c][1]], [E[0][1]])
                            self.act(E[1][0][:, cs], gbs[c][0], AF.Exp, [gbs[c][1]], [E[1][1]], scale=-1.0)
                        self.tt(rt_, r32, E[0][0], ALU.mult, [r32k, E[0][1]], [rtk])
                        self.tt(bt_, b32, E[1][0], ALU.mult, [b32k, E[1][1]], [btk])
                        self.tt(kt_, kd, E[1][0], ALU.mult, [kdk, E[1][1]], [ktk])
                        xbs = cums("lexc")
                        rbs = cums("lrem")
                        for c in range(2):
                            cs = slice(c * 512, (c + 1) * 512)
                            self.act(E[0][0][:, cs], xbs[c][0], AF.Exp, [xbs[c][1]], [E[0][1]])
                            self.act(E[1][0][:, cs], rbs[c][0], AF.Exp, [rbs[c][1]], [E[1][1]])
                        self.stt(at_, kk, -1.0, E[0][0], ALU.mult, ALU.mult, [kkk, E[0][1]], [atk])
                        self.tt(Bh, b32, E[1][0], ALU.mult, [b32k, E[1][1]], [Bhk])
                        self.tt(Kh, kd, E[1][0], ALU.mult, [kdk, E[1][1]], [Khk])
                        self.cutpoint(5)
                        if g == 0:
                            self.dump("at%d" % d, at_, atk, [128, D], BF16)
                            self.dump("rt%d" % d, rt_, rtk, [128, D], BF16)
                            self.dump("bt%d" % d, bt_, btk, [128, D], BF16)
                            self.dump("kt%d" % d, kt_, ktk, [128, D], BF16)
                            self.dump("Bh%d" % d, Bh, Bhk, [128, D], BF16)
                            self.dump("Kh%d" % d, Kh, Khk, [128, D], BF16)
                        for (src, srck, dstf, dstk) in ((at_, atk, lambda h0: arTv[:, h0:h0 + 8, 0, :], arTk), (rt_, rtk, lambda h0: arTv[:, h0:h0 + 8, 1, :], arTk),
                                                       (bt_, btk, lambda h0: bTv[:, h0:h0 + 8, :], bTk), (kt_, ktk, lambda h0: kTv[:, h0:h0 + 8, :], kTk)):
                            for half in range(2):
                                pb, pk = self.tbank(half)
                                for hh in range(8):
                                    h = half * 8 + hh
                                    self.tr(pb[0:64, hh * 128:(hh + 1) * 128], src[:, h * 64:(h + 1) * 64], [srck], [pk])
                                self.copy(dstf(half * 8), pb[0:64, :].rearrange("p (h t) -> p h t", h=8), [pk], [dstk], eng=("act" if half == 0 else "dve"))
                        self.cutpoint(6)
                        for hg in range(4):
                            for hh in range(4):
                                h = hg * 4 + hh
                                sb, sbk = self.bank(hh)
                                self.mm(sb[:, 0:256], bTv[:, h, :], arTv[:, h, :, :], True, True, [bTk, arTk], [sbk])
                                self.mm(sb[:, 256:512], kTv[:, h, :], arTv[:, h, :, :], True, True, [kTk, arTk], [sbk])
                            n4, n4k = self.bank(4)
                            for hh in range(4):
                                h = hg * 4 + hh
                                self.mm(n4[:, hh * 128:(hh + 1) * 128], arTv[:, h, 0, :], bTv[:, h, :], True, True, [arTk, bTk], [n4k])
                            for hh in range(4):
                                sb, sbk = self.bank(hh)
                                self.tt(SCv[:, hh, :], sb, self.cs["mask4_%d" % d][:], ALU.mult, [sbk, "cs_mask4_%d" % d], [SCk])
                            self.tt(Np[0][0].rearrange("p (h t) -> p h t", h=4), n4.rearrange("p (h t) -> p h t", h=4),
                                    self.cs["maskn_%d" % d][:].unsqueeze(1).to_broadcast([128, 4, 128]), ALU.mult, [n4k, "cs_maskn_%d" % d], [Np[0][1]])
                            self.cutpoint(7)
                            def Mj(jj, hh):
                                return (SCv[:, hh, 0:128], SCk) if jj == 0 else (Mp[jj][0][:, hh * 128:(hh + 1) * 128], Mp[jj][1])
                            for jj in range(5):
                                ba, bak = self.bank(jj % 2)
                                bb, bbk = self.bank(2 + jj % 2)
                                for hh in range(4):
                                    m_, mk_ = Mj(jj, hh)
                                    n_ = Np[jj][0][:, hh * 128:(hh + 1) * 128]
                                    self.mm(ba[:, hh * 128:(hh + 1) * 128], n_, m_, True, True, [Np[jj][1], mk_], [bak])
                                    if jj < 4:
                                        self.mm(bb[:, hh * 128:(hh + 1) * 128], m_, n_, True, True, [Np[jj][1], mk_], [bbk])
                                self.copy(Mp[jj + 1][0], ba, [bak], [Mp[jj + 1][1]], eng="act")
                                if jj < 4:
                                    self.copy(Np[jj + 1][0], bb, [bbk], [Np[jj + 1][1]], eng="dve")
                            self.cutpoint(8)
                            xb, xbk = self.bank(4)
                            for hh in range(4):
                                h = hg * 4 + hh
                                self.mm(xb[:, hh * 128:hh * 128 + 64], SCv[:, hh, 256:384], V[:, h * 64:(h + 1) * 64], True, True, [SCk, Vk], [xbk])
                            Xc, Xck = X[0]
                            Xcv = Xc.rearrange("p (h c) -> p h c", h=4)
                            self.copy(Xcv[:, :, 0:64], xb.rearrange("p (h c) -> p h c", h=4)[:, :, 0:64], [xbk], [Xck], eng="act")
                            self.copy(Xcv[:, :, 64:128], at_[:, hg * 256:(hg + 1) * 256].rearrange("p (h c) -> p h c", h=4), [atk], [Xck], eng="dve")
                            for jj in range(6):
                                xb, xbk = self.bank(4 + jj % 2)
                                Xc, Xck = X[jj % 2]
                                Xn, Xnk = X[(jj + 1) % 2]
                                for hh in range(4):
                                    m_, mk_ = Mj(jj, hh)
                                    self.mm(xb[:, hh * 128:(hh + 1) * 128], m_, Xc[:, hh * 128:(hh + 1) * 128], True, True, [mk_, Xck], [xbk])
                                self.tt(Xn, Xc, xb, ALU.add, [Xck, xbk], [Xnk])
                            Xf, Xfk = X[0]
                            Xfv = Xf.rearrange("p (h c) -> p h c", h=4)
                            self.cutpoint(9)
                            for q in range(2):
                                self.ts(X2v[:, :, q, 0:128], Xfv, self.cs["csel"][:, q:q + 1], None, ALU.mult, None, [Xfk, "cs_csel"], [NXk])
                                self.ts(X2v[:, :, q, 128:192], V[:, hg * 256:(hg + 1) * 256].rearrange("p (h c) -> p h c", h=4), self.cs["csel"][:, q:q + 1], None,
                                        ALU.mult, None, [Vk, "cs_csel"], [NXk])
                            gb_, gbk_ = self.bank(0)
                            pb_, pbk_ = self.bank(1)
                            for hh in range(4):
                                h = hg * 4 + hh
                                hc = slice(h * 64, (h + 1) * 64)
                                self.mm(gb_[0:64, hh * 128:(hh + 1) * 128], Bh[:, hc], X2v[:, hh, :, 0:64], True, False, [Bhk, NXk], [gbk_])
                                self.mm(gb_[0:64, hh * 128:(hh + 1) * 128], Kh[:, hc], X2v[:, hh, :, 128:192], False, True, [Khk, NXk], [gbk_])
                                for q in range(2):
                                    oc = slice((hh * 2 + q) * 64, (hh * 2 + q + 1) * 64)
                                    self.mm(pb_[0:64, oc], X2v[:, hh, q, 64:128], Bh[:, hc], True, True, [Bhk, NXk], [pbk_])
                            self.cutpoint(91)
                            self.copy(Gst[0:64, hg * 512:(hg + 1) * 512], gb_[0:64, :], [gbk_], [Gstk], eng="act")
                            self.cutpoint(92)
                            for hh in range(4):
                                h = hg * 4 + hh
                                for q in range(2):
                                    oc = slice((hh * 2 + q) * 64, (hh * 2 + q + 1) * 64)
                                    self.stt(PTv[0:64, h, q, :], self.identf[0:64, 0:64], etot[0:64, h * 2 + q:h * 2 + q + 1], pb_[0:64, oc], ALU.mult, ALU.add,
                                             ["identf", etk, pbk_], [PTsk])
                            self.cutpoint(10)
                            rb_, rbk_ = self.bank(2)
                            ob_, obk_ = self.bank(3)
                            for hh in range(4):
                                h = hg * 4 + hh
                                hc = slice(h * 64, (h + 1) * 64)
                                self.mm(rb_[0:64, hh * 128:(hh + 1) * 128], Xfv[:, hh, 64:128], SCv[:, hh, 128:256], True, False, [Xfk, SCk], [rbk_])
                                self.mm(rb_[0:64, hh * 128:(hh + 1) * 128], rt_[:, hc], self.identb[:], False, True, [rtk, "identb"], [rbk_])
                                self.mm(ob_[:, hh * 64:(hh + 1) * 64], SCv[:, hh, 128:256], Xfv[:, hh, 0:64], True, False, [SCk, Xfk], [obk_])
                                self.mm(ob_[:, hh * 64:(hh + 1) * 64], SCv[:, hh, 384:512], V[:, hc], False, True, [SCk, Vk], [obk_])
                            self.copy(RTs[0:64, hg * 512:(hg + 1) * 512], rb_[0:64, :], [rbk_], [RTsk], eng="act")
                            cs = slice(hg * 256, (hg + 1) * 256)
                            if d == 0:
                                self.copy(oacc[:, cs], ob_[:, 0:256], [obk_], [oacck], eng="dve")
                                recur(0, (0, 1), hg, oacc, oacck)
                            else:
                                self.tt(oacc[:, cs], oacc[:, cs], ob_[:, 0:256], ALU.add, [obk_, oacck], [oacck])
                        if d == 1:
                            self.store(self.PT[g], PTs[0:64, :], [PTsk], "r1pt")
                            self.store(self.GG[g], Gst[0:64, :], [Gstk], "r1gg")
                            self.store(self.RT[g], RTs[0:64, :], [RTsk], "r1rt")
                    if g == 0:
                        self.dump("oacc_r1", oacc, oacck, [128, D])
                        self.dump("bon", bon, bonk, [128, D])
                    self.store(self.OACC[r0:r0 + 128, :], oacc, [oacck], "r1oacc")
                    self.store(self.BONUS[r0:r0 + 128, :], bon, [bonk], "r1bon")
                except _Cut:
                    pass
        if getattr(self, "stop_after", "") == "R1":
            return
        self.reset()
        wo, wok = self.a16(KC * D, "rwo")
        wov = wo.rearrange("p (k c) -> p k c", k=KC)
        for k in range(KC):
            self.load(wov[:, k, :], self.w["rw_w_o"][j][k * 128:(k + 1) * 128, :], [wok], "w_rwo", eng="pool")
        lw_, lwk = self.bcload("lnx_w", self.w["rw_lnx_w"][j])
        lb_, lbk = self.bcload("lnx_b", self.w["rw_lnx_b"][j])
        x32, xk = self.a32(D, "x")
        junk, junkk = self.a32(D, "junk")
        st, stk = self.a32(96, "st")
        gt, gtk = self.a32(D, "gate")
        bon, bonk = self.a32(D, "bon")
        oacc, oacck = self.a32(D, "oacc")
        xo, xok = self.a32(D, "xo")
        Gst, Gstk = self.a16(16 * 128, "Gst")
        Gv = Gst[0:64, :].rearrange("p (h q v) -> p h q v", h=16, q=2)
        PTs, PTsk = self.a16(16 * 128, "PTs")
        PTv = PTs[0:64, :].rearrange("p (h q k) -> p h q k", h=16, q=2)
        RTs, RTsk = self.a16(16 * 128, "RTs")
        RTv = RTs[0:64, :].rearrange("p (h t) -> p h t", h=16)
        Hbf, Hbfk = self.a16(D, "Hbf")
        Hv = Hbf[0:64, :].rearrange("p (h v) -> p h v", h=16)
        on, onk = self.a16(D, "on")
        onT, onTk = self.a16(D, "onT")
        onTv = onT.rearrange("p (k t) -> p k t", k=KC)
        for si, tiles in enumerate(self.seq_tiles):
            zero_H()
            for g in reversed(tiles):
                r0 = g * 128
                self.load(PTs[0:64, :], self.PT[g], [PTsk], "r2pt")
                self.load(Gst[0:64, :], self.GG[g], [Gstk], "r2gg")
                self.load(RTs[0:64, :], self.RT[g], [RTsk], "r2rt")
                self.load(oacc, self.OACC[r0:r0 + 128, :], [oacck], "r2oa")
                self.load(bon, self.BONUS[r0:r0 + 128, :], [bonk], "r2bon")
                self.load(gt, self.GATE[r0:r0 + 128, :], [gtk], "r2g")
                self.load(x32, self.XS[r0:r0 + 128, :], [xk], "r2x")
                for hg in range(4):
                    recur(1, (1, 0), hg, oacc, oacck)
                if g == 0:
                    self.dump("oacc_r2", oacc, oacck, [128, D])
                o3 = oacc.rearrange("p (h n) -> p h n", h=16)
                self.reduce(st[:, 0:16], o3, [oacck], [stk])
                self.act(junk, oacc, AF.Square, [oacck], [junkk])
                self.reduce(st[:, 16:32], junk.rearrange("p (h n) -> p h n", h=16), [junkk], [stk])
                self.ts(st[:, 32:48], st[:, 0:16], 1.0 / 64, None, ALU.mult, None, [stk], [stk])
                self.tt(st[:, 48:64], st[:, 32:48], st[:, 32:48], ALU.mult, [stk], [stk])
                self.stt(st[:, 64:80], st[:, 16:32], 1.0 / 64, st[:, 48:64], ALU.mult, ALU.subtract, [stk], [stk])
                self.ts(st[:, 64:80], st[:, 64:80], RW_LN_EPS, None, ALU.add, None, [stk], [stk])
                self.act(st[:, 64:80], st[:, 64:80], AF.Sqrt, [stk], [stk])
                self.recip(st[:, 80:96], st[:, 64:80], [stk], [stk])
                self.tt(o3, o3, st[:, 32:48].unsqueeze(2).to_broadcast([128, 16, 64]), ALU.subtract, [oacck, stk], [oacck])
                self.tt(o3, o3, st[:, 80:96].unsqueeze(2).to_broadcast([128, 16, 64]), ALU.mult, [oacck, stk], [oacck])
                self.tt(oacc, oacc, lw_, ALU.mult, [oacck, lwk], [oacck])
                self.tt(oacc, oacc, lb_, ALU.add, [oacck, lbk], [oacck])
                self.tt(oacc, oacc, bon, ALU.add, [oacck, bonk], [oacck])
                self.tt(on, oacc, gt, ALU.mult, [oacck, gtk], [onk])
                self.transpose8(on, onk, onT, onTk, 0)
                for cg in range(2):
                    pb, pk = nb()
                    for k in range(KC):
                        self.mm(pb, onTv[:, k, :], wov[:, k, cg * 512:(cg + 1) * 512], k == 0, k == KC - 1, [onTk, wok], [pk])
                    self.tt(xo[:, cg * 512:(cg + 1) * 512], x32[:, cg * 512:(cg + 1) * 512], pb, ALU.add, [xk, pk], [xok])
                self.store(self.XS[r0:r0 + 128, :], xo, [xok], "r2xo")

    def lb_phase(self):
        self.reset()
        nc = self.nc
        self.LBS = nc.dram_tensor("LBS", [4, D], F32).ap()
        lg, lgk = self.a32(4 * D, "lg")
        lgv = lg.rearrange("p (l c) -> p l c", l=4)
        self.load(lg[0:1, :], self.w["hg_lb_logits"].rearrange("l c -> (l c)").partition_broadcast(1), [lgk], "lg")
        e, ek = self.a32(4 * D, "le")
        ev = e.rearrange("p (l c) -> p l c", l=4)
        sm, smk = self.a32(D, "lsum")
        o, ok = self.a32(4 * D, "lo")
        ov = o.rearrange("p (l c) -> p l c", l=4)
        self.act(e[0:1, :], lg[0:1, :], AF.Exp, [lgk], [ek])
        self.tt(sm[0:1, :], ev[0:1, 0, :], ev[0:1, 1, :], ALU.add, [ek], [smk])
        self.tt(sm[0:1, :], sm[0:1, :], ev[0:1, 2, :], ALU.add, [ek, smk], [smk])
        self.tt(sm[0:1, :], sm[0:1, :], ev[0:1, 3, :], ALU.add, [ek, smk], [smk])
        self.recip(sm[0:1, :], sm[0:1, :], [smk], [smk])
        self.tt(ov[0:1, 0, :], ev[0:1, 1, :], sm[0:1, :], ALU.mult, [ek, smk], [ok])
        self.tt(ov[0:1, 2, :], ev[0:1, 1, :], ev[0:1, 2, :], ALU.add, [ek], [ok])
        self.tt(ov[0:1, 2, :], ov[0:1, 2, :], ev[0:1, 3, :], ALU.add, [ek, ok], [ok])
        self.tt(ov[0:1, 2, :], ov[0:1, 2, :], sm[0:1, :], ALU.mult, [smk, ok], [ok])
        self.ts(ov[0:1, 1, :], ov[0:1, 0, :], -1.0, 1.0, ALU.mult, ALU.add, [ok], [ok])
        self.ts(ov[0:1, 3, :], ov[0:1, 2, :], -1.0, 1.0, ALU.mult, ALU.add, [ok], [ok])
        self.store(self.LBS.rearrange("l c -> (l c)").partition_broadcast(1), o[0:1, :], [ok], "lbs")

    def hgrn_layer(self, li, j):
        if not hasattr(self, "LBS"):
            self.lb_phase()
        S = self.S
        self.reset()
        win, wink = self.a16(KC * 5 * D, "hwin")
        winv = win.rearrange("p (k c) -> p k c", k=KC)
        for k in range(KC):
            for c5 in range(5):
                self.load(winv[:, k, c5 * D:(c5 + 1) * D], self.w["hg_w_in"][j][k * 128:(k + 1) * 128, c5 * D:(c5 + 1) * D], [wink], "w_hwin", eng="pool")
        nw, nwk = self.bcload("nmix", self.w["norm_mix"][li])
        lb, lbk = self.bcload("lb", self.LBS[2 * j])
        om, omk = self.bcload("omlb", self.LBS[2 * j + 1])
        x32, xk = self.a32(D, "x")
        junk, junkk = self.a32(D, "junk")
        st, stk = self.a32(4, "st")
        qs, qsk = self.a32(D, "qs")
        sgt, sgtk = self.a32(D, "sgate")
        sig = [self.a32(D, "sig%d" % d) for d in range(2)]
        kf, kfk = self.a32(D, "kf")
        logf, logfk = self.a32(D, "logf")
        E = [self.a32(D, "E%d" % i) for i in range(2)]
        S32, S32k = self.a32(D, "S32")
        oacc, oacck = self.a32(D, "oacc")
        etot = [self.a32(16, "etot%d" % d) for d in range(2)]
        xn, xnk = self.a16(D, "xn")
        xnT, xnTk = self.a16(D, "xnT")
        xnTv = xnT.rearrange("p (k t) -> p k t", k=KC)
        V, Vk = self.a16(D, "V")
        qt = [self.a16(D, "qt%d" % d) for d in range(2)]
        kt = [self.a16(D, "kt%d" % d) for d in range(2)]
        kh = [self.a16(D, "kh%d" % d) for d in range(2)]
        qT = [self.a16(D, "qT%d" % d) for d in range(2)]
        kT = [self.a16(D, "kT%d" % d) for d in range(2)]
        AT, ATk = self.a16(D, "AT")
        ATv = AT.rearrange("p (h t) -> p h t", h=8)
        Sbf, Sbfk = self.a16(D, "Sbf")
        rot = [0]

        def nb():
            rot[0] = (rot[0] + 1) % 4
            return self.bank(rot[0])

        def recur_tile(d, qTa, qTk, kha, khk, Va, Vka, eta, etk, first_chunk_of_seq, want_intra, ATv_, ATk_, acc_mode):
            qTv = qTa.rearrange("p (h t) -> p h t", h=8)
            order = (0, 1) if d == 0 else (1, 0)
            for hg in range(2):
                ob, obk = self.bank(4 + hg)
                for hh in range(4):
                    h = hg * 4 + hh
                    osl = ob[:, hh * 128:(hh + 1) * 128]
                    if want_intra:
                        self.mm(osl, ATv_[:, h, :], Va[:, h * 128:(h + 1) * 128], True, False, [ATk_, Vka], [obk])
                    for qi, q in enumerate(order):
                        ps = slice(64 * q, 64 * q + 64)
                        self.mm(ob[ps, hh * 128:(hh + 1) * 128], qTv[:, h, 64 * q:64 * q + 64], Sbf[:, h * 128:(h + 1) * 128],
                                not want_intra, qi == 1, [qTk, Sbfk + str(h)], [obk])
                        kvb, kvk = nb()
                        self.mm(kvb[:, 0:128], kha[ps, h * 128:(h + 1) * 128], Va[ps, h * 128:(h + 1) * 128], True, True, [khk, Vka], [kvk])
                        self.stt(S32[:, h * 128:(h + 1) * 128], S32[:, h * 128:(h + 1) * 128], eta[:, h * 2 + q:h * 2 + q + 1], kvb[:, 0:128],
                                 ALU.mult, ALU.add, [S32k + str(h), etk, kvk], [S32k + str(h)])
                        self.copy(Sbf[:, h * 128:(h + 1) * 128], S32[:, h * 128:(h + 1) * 128], [S32k + str(h)], [Sbfk + str(h)], eng="act")
                cs = slice(hg * 512, (hg + 1) * 512)
                if acc_mode == "set":
                    self.copy(oacc[:, cs], ob, [obk], [oacck], eng="act")
                else:
                    self.tt(oacc[:, cs], oacc[:, cs], ob, ALU.add, [obk, oacck], [oacck])

        def zero_state():
            self.memset(S32, 0.0, [S32k + str(h) for h in range(8)])
            self.memset(Sbf, 0.0, [Sbfk + str(h) for h in range(8)])

        for tiles in self.seq_tiles:
            zero_state()
            for g in tiles:
                r0 = g * 128
                self.load(x32, self.XS[r0:r0 + 128, :], [xk], "hx")
                self.rmsnorm_tile(x32, xk, nw, nwk, xn, xnk, junk, junkk, st, stk)
                self.transpose8(xn, xnk, xnT, xnTk, 0)
                for c in range(10):
                    pb, pk = nb()
                    for k in range(KC):
                        self.mm(pb, xnTv[:, k, :], winv[:, k, c * 512:(c + 1) * 512], k == 0, k == KC - 1, [xnTk, wink], [pk])
                    cs = slice((c % 2) * 512, (c % 2) * 512 + 512)
                    if c < 2:
                        self.act(qs[:, cs], pb, AF.Silu, [pk], [qsk])
                    elif c < 4:
                        self.copy(V[:, cs], pb, [pk], [Vk], eng="dve")
                    elif c < 6:
                        self.act(sgt[:, cs], pb, AF.Silu, [pk], [sgtk])
                    else:
                        sa, sk_ = sig[(c - 6) // 2]
                        self.act(sa[:, cs], pb, AF.Sigmoid, [pk], [sk_])
                self.store(self.GATE[r0:r0 + 128, :], sgt, [sgtk], "hgate")
                self.store(self.VV[r0:r0 + 128, :], V, [Vk], "hvv")
                for d in range(2):
                    sa, sk_ = sig[d]
                    self.tt(sa, sa, om, ALU.mult, [sk_, omk], [sk_])
                    self.tt(sa, sa, lb, ALU.add, [sk_, lbk], [sk_])
                    self.ts(kf, sa, -1.0, 1.0, ALU.mult, ALU.add, [sk_], [kfk])
                    self.act(logf, sa, AF.Ln, [sk_], [logfk])
                    gb = [self.bank(0), self.bank(1)]
                    rb = [self.bank(2), self.bank(3)]
                    for c in range(2):
                        cs = slice(c * 512, (c + 1) * 512)
                        self.mm(gb[c][0], self.cs["linc_%d" % d][:], logf[:, cs], True, True, ["cs_linc_%d" % d, logfk], [gb[c][1]])
                        self.mm(rb[c][0], self.cs["lrem_%d" % d][:], logf[:, cs], True, True, ["cs_lrem_%d" % d, logfk], [rb[c][1]])
                    tb, tbk = self.bank(4)
                    for h in range(8):
                        self.mm(tb[:, h * 2:h * 2 + 2], logf[:, h * 128:(h + 1) * 128], self.cs["csel"][:], True, True, [logfk, "cs_csel"], [tbk])
                    eta, etk = etot[d]
                    self.act(eta, tb[:, 0:16], AF.Exp, [tbk], [etk])
                    qta, qtk = qt[d]
                    kta, ktk = kt[d]
                    kha, khk = kh[d]
                    for c in range(2):
                        cs = slice(c * 512, (c + 1) * 512)
                        self.act(E[0][0][:, cs], gb[c][0], AF.Exp, [gb[c][1]], [E[0][1]])
                        self.act(E[1][0][:, cs], gb[c][0], AF.Exp, [gb[c][1]], [E[1][1]], scale=-1.0)
                    self.tt(qta, qs, E[0][0], ALU.mult, [qsk, E[0][1]], [qtk])
                    self.tt(kta, kf, E[1][0], ALU.mult, [kfk, E[1][1]], [ktk])
                    for c in range(2):
                        cs = slice(c * 512, (c + 1) * 512)
                        self.act(E[0][0][:, cs], rb[c][0], AF.Exp, [rb[c][1]], [E[0][1]])
                    self.tt(kha, kf, E[0][0], ALU.mult, [kfk, E[0][1]], [khk])
                    self.transpose8(qta, qtk, qT[d][0], qT[d][1], 1)
                    self.transpose8(kta, ktk, kT[d][0], kT[d][1], 0)
                    qTv = qT[d][0].rearrange("p (h t) -> p h t", h=8)
                    kTv = kT[d][0].rearrange("p (h t) -> p h t", h=8)
                    for hg in range(2):
                        sb, sbk = nb()
                        for hh in range(4):
                            h = hg * 4 + hh
                            self.mm(sb[:, hh * 128:(hh + 1) * 128], kTv[:, h, :], qTv[:, h, :], True, True, [kT[d][1], qT[d][1]], [sbk])
                        self.tt(ATv[:, hg * 4:(hg + 1) * 4, :], sb.rearrange("p (h t) -> p h t", h=4),
                                self.cs["maski_%d" % d][:].unsqueeze(1).to_broadcast([128, 4, 128]), ALU.mult, [sbk, "cs_maski_%d" % d], [ATk])
                    if d == 0:
                        recur_tile(0, qT[0][0], qT[0][1], kha, khk, V, Vk, eta, etk, False, True, ATv, ATk, "set")
                    else:
                        for hg in range(2):
                            ob, obk = self.bank(4 + hg)
                            for hh in range(4):
                                h = hg * 4 + hh
                                self.mm(ob[:, hh * 128:(hh + 1) * 128], ATv[:, h, :], V[:, h * 128:(h + 1) * 128], True, True, [ATk, Vk], [obk])
                            cs = slice(hg * 512, (hg + 1) * 512)
                            self.tt(oacc[:, cs], oacc[:, cs], ob, ALU.add, [obk, oacck], [oacck])
                        self.store(self.QT[g], qT[1][0], [qT[1][1]], "hqt")
                        self.store(self.KH[r0:r0 + 128, :], kha, [khk], "hkh")
                        self.store(self.ETOT[g], eta, [etk], "hetot")
                if li == 3 and g in (0, 15):
                    self.dump("h1_x_%d" % g, x32, xk, [128, D])
                    self.dump("h1_qs_%d" % g, qs, qsk, [128, D])
                    self.dump("h1_lb_%d" % g, lb, lbk, [128, D])
                    self.dump("h1_sig0_%d" % g, sig[0][0], sig[0][1], [128, D])
                    self.dump("h1_E0_%d" % g, E[0][0], E[0][1], [128, D])
                    self.dump("h1_S32_%d" % g, S32, [S32k + str(h) for h in range(8)], [128, D])
                    self.dump("h1_oacc_%d" % g, oacc, oacck, [128, D])
                self.store(self.OACC[r0:r0 + 128, :], oacc, [oacck], "hoacc")
        self.reset()
        wo, wok = self.a16(KC * D, "hwo")
        wov = wo.rearrange("p (k c) -> p k c", k=KC)
        for k in range(KC):
            self.load(wov[:, k, :], self.w["hg_w_o"][j][k * 128:(k + 1) * 128, :], [wok], "w_hwo", eng="pool")
        gn, gnk = self.bcload("gnorm", self.w["hg_gnorm"][j])
        x32, xk = self.a32(D, "x")
        junk, junkk = self.a32(D, "junk")
        st, stk = self.a32(32, "st")
        sgt, sgtk = self.a32(D, "sgate")
        S32, S32k = self.a32(D, "S32")
        oacc, oacck = self.a32(D, "oacc")
        eta, etk = self.a32(16, "etot")
        xo, xok = self.a32(D, "xo")
        qTa, qTk = self.a16(D, "qT")
        kha, khk = self.a16(D, "kh")
        V, Vk = self.a16(D, "V")
        Sbf, Sbfk = self.a16(D, "Sbf")
        on, onk = self.a16(D, "on")
        onT, onTk = self.a16(D, "onT")
        onTv = onT.rearrange("p (k t) -> p k t", k=KC)
        for tiles in self.seq_tiles:
            zero_state()
            for g in reversed(tiles):
                r0 = g * 128
                self.load(qTa, self.QT[g], [qTk], "h2qt")
                self.load(kha, self.KH[r0:r0 + 128, :], [khk], "h2kh")
                self.load(V, self.VV[r0:r0 + 128, :], [Vk], "h2vv")
                self.load(eta, self.ETOT[g], [etk], "h2et")
                self.load(oacc, self.OACC[r0:r0 + 128, :], [oacck], "h2oa")
                self.load(sgt, self.GATE[r0:r0 + 128, :], [sgtk], "h2g")
                self.load(x32, self.XS[r0:r0 + 128, :], [xk, "XS%d" % g], "h2x")
                if g == 0 and li == 3:
                    self.dump("h2_oacc_in", oacc, oacck, [128, D])
                    self.dump("h2_eta", eta, etk, [128, 16])
                    self.dump("h2_S32_in", S32, [S32k + str(h) for h in range(8)], [128, D])
                    self.dump("h2_qT", qTa, qTk, [128, D], BF16)
                    self.dump("h2_kh", kha, khk, [128, D], BF16)
                recur_tile(1, qTa, qTk, kha, khk, V, Vk, eta, etk, False, False, None, None, "add")
                if g == 0:
                    self.dump("h2_oacc_out", oacc, oacck, [128, D])
                self.act(junk, oacc, AF.Square, [oacck], [junkk])
                self.reduce(st[:, 0:8], junk.rearrange("p (h v) -> p h v", h=8), [junkk], [stk])
                self.ts(st[:, 8:16], st[:, 0:8], 1.0 / 128, NORM_EPS, ALU.mult, ALU.add, [stk], [stk])
                self.act(st[:, 16:24], st[:, 8:16], AF.Sqrt, [stk], [stk])
                self.recip(st[:, 24:32], st[:, 16:24], [stk], [stk])
                self.tt(oacc.rearrange("p (h v) -> p h v", h=8), oacc.rearrange("p (h v) -> p h v", h=8),
                        st[:, 24:32].unsqueeze(2).to_broadcast([128, 8, 128]), ALU.mult, [oacck, stk], [oacck])
                self.tt(oacc, oacc, gn, ALU.mult, [oacck, gnk], [oacck])
                self.tt(on, oacc, sgt, ALU.mult, [oacck, sgtk], [onk])
                self.transpose8(on, onk, onT, onTk, 0)
                for cg in range(2):
                    pb, pk = nb()
                    for k in range(KC):
                        self.mm(pb, onTv[:, k, :], wov[:, k, cg * 512:(cg + 1) * 512], k == 0, k == KC - 1, [onTk, wok], [pk])
                    self.tt(xo[:, cg * 512:(cg + 1) * 512], x32[:, cg * 512:(cg + 1) * 512], pb, ALU.add, [xk, pk], [xok])
                if li == 3 and g in (0, 15):
                    self.dump("h2_xo_%d" % g, xo, xok, [128, D])
                self.store(self.XS[r0:r0 + 128, :], xo, [xok], "h2xo", writes=["XS%d" % g])

    def copy_in(self):
        self.reset()
        xt = [self.a32(4 * D, "cx%d" % i) for i in range(2)]
        nsup = (self.NT + 3) // 4
        for su in range(nsup):
            r0 = su * 512
            n = min(self.T - r0, 512)
            xa, xak = xt[su % 2]
            xav = xa.rearrange("p (i c) -> p i c", i=4)
            self.load(xav[:, 0:n // 128, :], self.x_in[r0:r0 + n, :].rearrange("(i p) c -> p i c", p=128), [xak], "cx%d" % (su % 2))
            self.store(self.XS[r0:r0 + n, :].rearrange("(i p) c -> p i c", p=128), xav[:, 0:n // 128, :], [xak], "cxs%d" % (su % 2))

    def final_phase(self):
        self.reset()
        nw, nwk = self.bcload("nfin", self.w["norm_final"])
        xt = [self.a32(D, "fx%d" % i) for i in range(2)]
        ot = [self.a32(D, "fo%d" % i) for i in range(2)]
        st, stk = self.a32(4, "st")
        junk, junkk = self.a32(D, "junk")
        for g in range(self.NT):
            xa, xak = xt[g % 2]
            oa, oak = ot[g % 2]
            self.load(xa, self.XS[g * 128:(g + 1) * 128, :], [xak], "fx%d" % (g % 2))
            if self.final_norm:
                self.rmsnorm_tile(xa, xak, nw, nwk, oa, oak, junk, junkk, st, stk)
            else:
                self.copy(oa, xa, [xak], [oak])
            self.store(self.y_out[g * 128:(g + 1) * 128, :], oa, [oak], "fo%d" % (g % 2))

    def build(self):
        nc = self.nc
        self.XS3 = nc.dram_tensor("XS3", [self.T, D], F32).ap()
        self.load_consts()
        self.copy_in()
        ri = 0
        hi = 0
        for li, kind in enumerate(self.layers):
            if kind == "rwkv":
                from_rw = getattr(self, "rwkv_layer", None)
                if from_rw is not None:
                    from_rw(li, li // 2)
            elif kind == "hgrn":
                from_hg = getattr(self, "hgrn_layer", None)
                if from_hg is not None:
                    from_hg(li, li // 2)
            elif kind == "ffn":
                pass
            if kind != "none":
                self.ffn_layer(li)
        self.final_phase()
        self.S.finalize()
        return nc


def build_program(seq_lens, layers, final_norm=True, debug=False):
    b = Builder(seq_lens, layers, final_norm)
    b.debug = debug
    import os
    b.stop_after = os.environ.get("RW_STOP", "")
    b.cut = int(os.environ.get("RW_CUT", "0"))
    nc = b.build()
    return nc, b


def run_cores(nc, x_cores, weights):
    consts = _consts()
    in_maps = []
    for xc in x_cores:
        m = {"x": np.ascontiguousarray(xc, dtype=np.float32)}
        for k in WEIGHT_SHAPES:
            m[k] = np.ascontiguousarray(weights[k], dtype=np.float32)
        for k, v in consts.items():
            m["c_" + k] = v
        in_maps.append(m)
    res = run_bass_kernel_spmd(nc, in_maps, core_ids=list(range(len(x_cores))))
    global LAST_RESULTS
    LAST_RESULTS = res.results
    return [r["y"] for r in res.results]


def kernel(**inputs):
    xp = np.asarray(inputs["x_prompt"], dtype=np.float32)
    xs = np.asarray(inputs["x_sample"], dtype=np.float32)
    seq_lens = [2048, 2048, 8192]
    nc, _ = build_program(seq_lens, ["rwkv", "hgrn", "rwkv", "hgrn"])
    x_cores = [np.concatenate([xp[2 * c], xp[2 * c + 1], xs[c]], axis=0) for c in range(NCORES)]
    weights = {k: np.asarray(inputs[k], dtype=np.float32) for k in WEIGHT_SHAPES}
    ys = run_cores(nc, x_cores, weights)
    y_prompt = np.stack([ys[c // 2][(c % 2) * 2048:(c % 2 + 1) * 2048] for c in range(16)], axis=0)
    y_sample = np.stack([ys[c][4096:] for c in range(NCORES)], axis=0)
    return (y_prompt.astype(np.float32), y_sample.astype(np.float32))
```

```python
import contextlib
import math
from collections import defaultdict

import numpy as np
import concourse.bass as bass
import concourse.mybir as mybir
from concourse.bass_utils import run_bass_kernel_spmd

F32 = mybir.dt.float32
BF16 = mybir.dt.bfloat16
AF = mybir.ActivationFunctionType
ALU = mybir.AluOpType
AX = mybir.AxisListType

SAME_ENGINE_SYNC = True
ATTACH_WAITS = True

D = 1024
KC = 8
FF = 2816
NCORES = 8
RW_LN_EPS = 64e-5
NORM_EPS = 1e-6
DECAY_C = -math.exp(-0.5)


class _Op:
    __slots__ = ("eng", "fn", "deps", "needed", "val", "semkey", "is_dma", "waits", "is_nop")


class Sched:
    ENG = ("pe", "act", "dve", "pool", "sp")

    def __init__(self, nc):
        self.nc = nc
        self.ops = []
        self.last_w = {}
        self.readers = {}
        self.stack = contextlib.ExitStack()
        self.last_eng = {}
        self.last_dma = {}

    def sbuf(self, name, shape, dtype):
        return self.stack.enter_context(self.nc.sbuf_tensor(name, list(shape), dtype))

    def psum(self, name, shape, dtype):
        return self.stack.enter_context(self.nc.psum_tensor(name, list(shape), dtype))

    def _add(self, eng, fn, reads, writes, is_dma, semkey, extra=None):
        deps = set()
        lw = self.last_w
        rd = self.readers
        for k in reads:
            w = lw.get(k)
            if w is not None:
                deps.add(w)
        for k in writes:
            w = lw.get(k)
            if w is not None:
                deps.add(w)
            r = rd.get(k)
            if r:
                deps.update(r)
        if extra:
            deps.update(extra)
        i = len(self.ops)
        op = _Op()
        op.eng = eng
        op.fn = fn
        op.deps = deps
        op.needed = is_dma
        op.val = 0
        op.semkey = semkey
        op.is_dma = is_dma
        op.waits = None
        op.is_nop = False
        self.ops.append(op)
        for k in reads:
            rd.setdefault(k, []).append(i)
        for k in writes:
            lw[k] = i
            rd[k] = []
        if is_dma:
            self.last_dma[semkey] = i
        else:
            self.last_eng[eng] = i
        return i

    def op(self, eng, fn, reads=(), writes=()):
        return self._add(eng, fn, reads, writes, False, eng)

    def dma(self, eng, out, in_, reads=(), writes=(), sem=None):
        return self._add(eng, lambda e: e.dma_start(out=out, in_=in_), reads, writes, True, "dma_" + sem)

    def barrier(self):
        extra = set(self.last_eng.values()) | set(self.last_dma.values())
        for eng in self.ENG:
            i = self._add(eng, lambda e: e.nop(), (), (), False, eng, extra=extra)
            self.ops[i].is_nop = True
        self.last_w = {}
        self.readers = {}

    def finalize(self):
        nc = self.nc
        ops = self.ops
        for op in ops:
            for d in op.deps:
                ops[d].needed = True
        count = defaultdict(int)
        known = {e: defaultdict(int) for e in self.ENG}
        per_eng = {e: [] for e in self.ENG}
        for op in ops:
            waits = {}
            kn = known[op.eng]
            for d in op.deps:
                dop = ops[d]
                if dop.is_dma:
                    sk = dop.semkey
                    val = count[sk] * 16
                else:
                    sk = dop.eng
                    val = dop.val
                    if dop.eng == op.eng and not op.is_dma:
                        if op.eng in ("pe", "sp") or not SAME_ENGINE_SYNC:
                            continue
                if kn[sk] >= val:
                    continue
                if waits.get(sk, 0) < val:
                    waits[sk] = val
            for sk, val in waits.items():
                kn[sk] = val
            op.waits = waits
            if op.needed:
                count[op.semkey] += 1
                op.val = count[op.semkey] * (16 if op.is_dma else 1)
            per_eng[op.eng].append(op)
        semkeys = sorted(count.keys())
        self.sem_final = {sk: count[sk] * (16 if sk.startswith("dma_") else 1) for sk in semkeys}
        sems = {}
        for sk in semkeys:
            sems[sk] = self.stack.enter_context(nc.semaphore("s_" + sk))
        self.n_sems = len(semkeys)
        final_waits = {sk: count[sk] * 16 for sk in semkeys if sk.startswith("dma_")}
        block = self.stack.enter_context(nc.Block())

        def emit(eng_name):
            lst = per_eng[eng_name]

            def body(e):
                attach = ATTACH_WAITS and eng_name in ("pe", "act", "dve")
                for op in lst:
                    wl = list(op.waits.items())
                    last = None
                    if attach and wl and not op.is_dma and not op.is_nop:
                        last = wl.pop()
                    for sk, val in wl:
                        e.wait_ge(sems[sk], val)
                    ins = op.fn(e)
                    if last is not None:
                        ins._wait_ge(sems[last[0]], last[1])
                    if op.needed:
                        ins.then_inc(sems[op.semkey], 16 if op.is_dma else 1)
                if eng_name == "pool":
                    for sk, val in final_waits.items():
                        e.wait_ge(sems[sk], val)

            return body

        block.tensor(emit("pe"))
        block.scalar(emit("act"))
        block.vector(emit("dve"))
        block.gpsimd(emit("pool"))
        block.sync(emit("sp"))
        self.counts = {e: len(per_eng[e]) for e in self.ENG}
        self.stack.close()


def _consts():
    idx = np.arange(128)
    s = idx[:, None]
    t = idx[None, :]
    blk = (s // 64) == (t // 64)
    c = {}
    c["ident"] = np.eye(128, dtype=np.float32)
    for d in (0, 1):
        strict = blk & ((s < t) if d == 0 else (s > t))
        incl = blk & ((s <= t) if d == 0 else (s >= t))
        after = blk & ((s > t) if d == 0 else (s < t))
        c["mask4_%d" % d] = np.concatenate([strict, incl, strict, incl], axis=1).astype(np.float32)
        c["maskn_%d" % d] = strict.T.astype(np.float32)
        c["maski_%d" % d] = incl.astype(np.float32)
        c["linc_%d" % d] = incl.astype(np.float32)
        c["lrem_%d" % d] = after.astype(np.float32)
    csel = np.zeros((128, 2), np.float32)
    csel[:64, 0] = 1.0
    csel[64:, 1] = 1.0
    c["csel"] = csel
    c["rcsel"] = (DECAY_C * csel).astype(np.float32)
    for d in (0, 1):
        for nm, mm_ in (("linc", (s <= t) if d == 0 else (s >= t)), ("lexc", (s < t) if d == 0 else (s > t)), ("lrem", (s > t) if d == 0 else (s < t))):
            c["r%s_%d" % (nm, d)] = (DECAY_C * (blk & mm_)).astype(np.float32)
    c["ones_row"] = np.ones((1, 128), np.float32)
    sel = np.zeros((5, 5 * 128), np.float32)
    for i in range(5):
        sel[i, i * 128:(i + 1) * 128] = 1.0
    c["sel5"] = sel
    return c


CONST_SHAPES = {k: v.shape for k, v in _consts().items()}

WEIGHT_SHAPES = {
    "norm_mix": (4, D), "norm_ffn": (4, D), "norm_final": (D,), "rw_mu": (2, 6, D),
    "rw_w_rkv": (2, 3, D, D), "rw_w0": (2, 2, D), "rw_w1": (2, 2, D, 64), "rw_w2": (2, 2, 64, D),
    "rw_a0": (2, 2, D), "rw_a1": (2, 2, D, 64), "rw_a2": (2, 2, 64, D), "rw_v0": (1, D),
    "rw_v1": (1, D, 32), "rw_v2": (1, 32, D), "rw_g1": (2, D, 160), "rw_g2": (2, 160, D),
    "rw_k_k": (2, D), "rw_k_a": (2, D), "rw_r_k": (2, D), "rw_lnx_w": (2, D), "rw_lnx_b": (2, D),
    "rw_w_o": (2, D, D), "hg_w_in": (2, D, 5 * D), "hg_lb_logits": (4, D), "hg_gnorm": (2, D),
    "hg_w_o": (2, D, D), "ffn_w_in": (4, D, 2 * FF), "ffn_w_out": (4, FF, D),
}


class _Cut(Exception):
    pass


class Builder:
    def cutpoint(self, n):
        if getattr(self, "cut", 0) == n:
            raise _Cut()

    def __init__(self, seq_lens, layers, final_norm=True):
        self.seq_lens = list(seq_lens)
        self.layers = list(layers)
        self.final_norm = final_norm
        self.T = sum(seq_lens)
        self.NT = self.T // 128
        nc = bass.Bass("TRN2", target_bir_lowering=False)
        self.nc = nc
        self.S = Sched(nc)
        S = self.S
        T = self.T
        self.x_in = nc.dram_tensor("x", [T, D], F32, kind="ExternalInput").ap()
        self.y_out = nc.dram_tensor("y", [T, D], F32, kind="ExternalOutput").ap()
        self.w = {k: nc.dram_tensor(k, list(s), F32, kind="ExternalInput").ap() for k, s in WEIGHT_SHAPES.items()}
        self.cin = {k: nc.dram_tensor("c_" + k, list(s), F32, kind="ExternalInput").ap() for k, s in CONST_SHAPES.items()}
        self.XS = nc.dram_tensor("XS", [T, D], F32).ap()
        self.XS2 = nc.dram_tensor("XS2", [T, D], F32).ap()
        nseq = len(self.seq_lens)
        self.HIN = nc.dram_tensor("HIN", [T + 2 * nseq, D], BF16).ap()
        self.OACC = nc.dram_tensor("OACC", [T, D], F32).ap()
        self.GATE = nc.dram_tensor("GATE", [T, D], F32).ap()
        self.BONUS = nc.dram_tensor("BONUS", [T, D], F32).ap()
        self.VFIRST = nc.dram_tensor("VFIRST", [T, D], F32).ap()
        self.QT = nc.dram_tensor("QT", [self.NT, 128, D], BF16).ap()
        self.KH = nc.dram_tensor("KH", [T, D], BF16).ap()
        self.VV = nc.dram_tensor("VV", [T, D], BF16).ap()
        self.ETOT = nc.dram_tensor("ETOT", [self.NT, 128, 16], F32).ap()
        self.PT = nc.dram_tensor("PT", [self.NT, 64, 16 * 128], BF16).ap()
        self.GG = nc.dram_tensor("GG", [self.NT, 64, 16 * 128], BF16).ap()
        self.RT = nc.dram_tensor("RT", [self.NT, 64, 16 * 128], BF16).ap()
        self.identb = S.sbuf("identb", [128, 128], BF16)
        self.identf = S.sbuf("identf", [128, 128], F32)
        self.cs = {}
        for k in CONST_SHAPES:
            if k in ("ident", "sel5", "ones_row"):
                continue
            shp = CONST_SHAPES[k]
            self.cs[k] = S.sbuf("cs_" + k, list(shp), BF16 if k.startswith("mask") else F32)
        self.sel5 = S.sbuf("sel5", [5, 640], BF16)
        self.A16_N = 68400
        self.A32_N = 15660
        self.A16 = S.sbuf("A16", [128, self.A16_N], BF16)
        self.A32 = S.sbuf("A32", [128, self.A32_N], F32)
        self.PS = S.psum("PS", [128, 6 * 512], F32)
        self.PT16 = S.psum("PT16", [128, 2 * 1024], BF16)
        self.p16 = 0
        self.p32 = 0
        self.phase_id = 0
        self.dbg = {}
        self.sem_map = {}
        self.sem_next = {}
        self.seq_tiles = []
        r = 0
        for si, L in enumerate(self.seq_lens):
            self.seq_tiles.append([(r // 128 + i) for i in range(L // 128)])
            r += L

    def reset(self):
        self.S.barrier()
        self.p16 = 0
        self.p32 = 0
        self.phase_id += 1

    def a16(self, n, name):
        assert self.p16 + n <= self.A16_N, ("A16 overflow", name, self.p16, n)
        ap = self.A16[:, self.p16:self.p16 + n]
        self.p16 += n
        return ap, "%s@%d" % (name, self.phase_id)

    def a32(self, n, name):
        assert self.p32 + n <= self.A32_N, ("A32 overflow", name, self.p32, n)
        ap = self.A32[:, self.p32:self.p32 + n]
        self.p32 += n
        return ap, "%s@%d" % (name, self.phase_id)

    def bank(self, i):
        return self.PS[:, i * 512:(i + 1) * 512], "psb%d" % i

    def tbank(self, i):
        return self.PT16[:, i * 1024:(i + 1) * 1024], "ptb%d" % i

    def mm(self, out, lhsT, rhs, start, stop, reads, writes):
        self.S.op("pe", lambda e: e.matmul(out, lhsT, rhs, start=start, stop=stop), reads, writes)

    def tr(self, out, in_, reads, writes):
        idt = self.identb[0:in_.shape[0], 0:in_.shape[0]]
        self.S.op("pe", lambda e: e.transpose(out, in_, idt), list(reads) + ["identb"], writes)

    def act(self, out, in_, func, reads, writes, **kw):
        self.S.op("act", lambda e: e.activation(out=out, in_=in_, func=func, **kw), reads, writes)

    def tt(self, out, in0, in1, op, reads, writes, eng="dve"):
        self.S.op(eng, lambda e: e.tensor_tensor(out=out, in0=in0, in1=in1, op=op), reads, writes)

    def stt(self, out, in0, scalar, in1, op0, op1, reads, writes, eng="dve"):
        self.S.op(eng, lambda e: e.scalar_tensor_tensor(out=out, in0=in0, scalar=scalar, in1=in1, op0=op0, op1=op1), reads, writes)

    def ts(self, out, in0, s1, s2, op0, op1, reads, writes, eng="dve"):
        if s2 is None:
            self.S.op(eng, lambda e: e.tensor_scalar(out=out, in0=in0, scalar1=s1, scalar2=None, op0=op0), reads, writes)
        else:
            self.S.op(eng, lambda e: e.tensor_scalar(out=out, in0=in0, scalar1=s1, scalar2=s2, op0=op0, op1=op1), reads, writes)

    def reduce(self, out, in_, reads, writes):
        self.S.op("dve", lambda e: e.tensor_reduce(out=out, in_=in_, axis=AX.X, op=ALU.add), reads, writes)

    def recip(self, out, in_, reads, writes):
        self.S.op("dve", lambda e: e.reciprocal(out=out, in_=in_), reads, writes)

    def memset(self, ap, val, writes):
        self.S.op("dve", lambda e: e.memset(ap, val), [], writes)

    def copy(self, out, in_, reads, writes, eng="dve"):
        if eng == "act":
            self.act(out, in_, AF.Copy, reads, writes)
        else:
            self.S.op(eng, lambda e: e.tensor_copy(out=out, in_=in_), reads, writes)

    def _semname(self, sem):
        key = (self.phase_id, sem)
        m = self.sem_map
        if key not in m:
            n = self.sem_next.get(self.phase_id, 0)
            self.sem_next[self.phase_id] = n + 1
            m[key] = "q%d" % n
        return m[key]

    def load(self, out, in_, writes, sem, reads=(), eng="sp"):
        self.S.dma(eng, out, in_, reads=reads, writes=writes, sem=self._semname(sem))

    def store(self, out, in_, reads, sem, writes=(), eng="pool"):
        self.S.dma(eng, out, in_, reads=reads, writes=writes, sem=self._semname(sem))

    def dump(self, name, ap, key, shape, dtype=F32):
        if not getattr(self, "debug", False):
            return
        if name in self.dbg:
            return
        t = self.nc.dram_tensor("dbg_" + name, list(shape), dtype, kind="ExternalOutput").ap()
        self.dbg[name] = t
        self.store(t, ap, [key] if isinstance(key, str) else key, "dbg_" + name)

    def load_consts(self):
        S = self.S
        self.load(self.identb[:], self.cin["ident"][:, :], ["identb"], "identb", eng="pool")
        self.load(self.identf[:], self.cin["ident"][:, :], ["identf"], "identf")
        for k, t in self.cs.items():
            self.load(t[:], self.cin[k][:, :], ["cs_" + k], "cs_" + k, eng=("pool" if k.startswith("mask") else "sp"))
        self.load(self.sel5[:], self.cin["sel5"][:, :], ["sel5"], "sel5", eng="pool")

    def wload(self, name, n, src_ap, shape_str=None, **kw):
        ap, key = self.a16(n, name)
        dst = ap if shape_str is None else ap.rearrange(shape_str, **kw)
        self.load(dst, src_ap, [key], "w_" + name, eng="pool")
        return ap, key

    def bcload(self, name, src_row):
        ap, key = self.a32(D, name)
        self.load(ap, src_row.partition_broadcast(128), [key], "bc_" + name)
        return ap, key

    def rmsnorm_tile(self, x_ap, xk, wbc, wk, out_ap, outk, junk, junkk, st, stk):
        self.act(junk, x_ap, AF.Square, [xk], [junkk])
        self.reduce(st[:, 0:1], junk, [junkk], [stk])
        self.ts(st[:, 1:2], st[:, 0:1], 1.0 / D, NORM_EPS, ALU.mult, ALU.add, [stk], [stk])
        self.act(st[:, 2:3], st[:, 1:2], AF.Sqrt, [stk], [stk])
        self.recip(st[:, 3:4], st[:, 2:3], [stk], [stk])
        self.stt(out_ap, x_ap, st[:, 3:4], wbc, ALU.mult, ALU.mult, [xk, stk, wk], [outk])

    def transpose8(self, src, srck, dst, dstk, tb, evac_eng="act"):
        pb, pk = self.tbank(tb)
        for k in range(KC):
            self.tr(pb[:, k * 128:(k + 1) * 128], src[:, k * 128:(k + 1) * 128], [srck], [pk])
        self.copy(dst, pb, [pk], [dstk], eng=evac_eng)

    def ffn_phase(self, li, half, src, dst):
        self.reset()
        HB = 11
        h0 = half * HB * 128
        win, wink = self.a16(KC * 2 * HB * 128, "win")
        winv = win.rearrange("p (k c) -> p k c", k=KC)
        wsrc = self.w["ffn_w_in"][li].rearrange("(k p) c -> p k c", p=128)
        for k in range(KC):
            self.load(winv[:, k, 0:HB * 128], wsrc[:, k, h0:h0 + HB * 128], [wink + "g"], "w_wing", eng="pool")
            self.load(winv[:, k, HB * 128:2 * HB * 128], wsrc[:, k, FF + h0:FF + h0 + HB * 128], [wink + "u"], "w_winu", eng="pool")
        wout, woutk = self.a16(HB * D, "wout")
        woutv = wout.rearrange("p (j c) -> p j c", j=HB)
        for j in range(HB):
            self.load(woutv[:, j, :], self.w["ffn_w_out"][li][h0 + j * 128:h0 + (j + 1) * 128, :], [woutk], "w_wout", eng="pool")
        nw, nwk = self.bcload("nffn", self.w["norm_ffn"][li])
        NB = 2
        xt = [self.a32(4 * D, "xt%d" % i) for i in range(NB)]
        st, stk = self.a32(4, "st")
        junk, junkk = self.a32(D, "junk")
        xn, xnk = self.a16(D, "xn")
        xnT = [self.a16(KC * 512, "xnT%d" % i) for i in range(NB)]
        hT, hTk = self.a16(HB * 512, "hT")
        hTv = hT.rearrange("p (j t) -> p j t", j=HB)
        sg = [self.a32(512, "sg%d" % i) for i in range(2)]
        ot = [self.a32(D, "ot%d" % i) for i in range(2)]
        nsup = (self.NT + 3) // 4
        for su in range(nsup):
            tiles = list(range(su * 4, min(self.NT, su * 4 + 4)))
            nt = len(tiles)
            ntok = nt * 128
            xa, xak = xt[su % NB]
            xav = xa.rearrange("p (i c) -> p i c", i=4)
            xTa, xTk = xnT[su % NB]
            xTv = xTa.rearrange("p (k t) -> p k t", k=KC)
            r0 = tiles[0] * 128
            self.load(xav[:, 0:nt, :], self.XS[r0:r0 + ntok, :].rearrange("(i p) c -> p i c", p=128), [xak], "xt%d" % (su % NB))
            for i in range(nt):
                self.rmsnorm_tile(xav[:, i, :], xak, nw, nwk, xn, xnk, junk, junkk, st, stk)
                pb, pk = self.tbank(i % 2)
                for k in range(KC):
                    self.tr(pb[:, k * 128:(k + 1) * 128], xn[:, k * 128:(k + 1) * 128], [xnk], [pk])
                self.copy(xTv[:, :, i * 128:(i + 1) * 128], pb.rearrange("p (k t) -> p k t", k=KC), [pk], [xTk], eng="act")
            for j in range(HB):
                gb, gk = self.bank((2 * j) % 4)
                ub, uk = self.bank((2 * j + 1) % 4)
                for k in range(KC):
                    self.mm(gb[:, 0:ntok], winv[:, k, j * 128:(j + 1) * 128], xTv[:, k, 0:ntok], k == 0, k == KC - 1, [wink + "g", xTk], [gk])
                for k in range(KC):
                    self.mm(ub[:, 0:ntok], winv[:, k, (HB + j) * 128:(HB + j + 1) * 128], xTv[:, k, 0:ntok], k == 0, k == KC - 1, [wink + "u", xTk], [uk])
                sga, sgk = sg[j % 2]
                self.act(sga[:, 0:ntok], gb[:, 0:ntok], AF.Silu, [gk], [sgk])
                self.tt(hTv[:, j, 0:ntok], sga[:, 0:ntok], ub[:, 0:ntok], ALU.mult, [sgk, uk], [hTk])
            if src is not self.XS:
                self.load(xav[:, 0:nt, :], src[r0:r0 + ntok, :].rearrange("(i p) c -> p i c", p=128), [xak], "xt%d" % (su % NB))
            for i in range(nt):
                oa, oak = ot[i % 2]
                for cg in range(2):
                    ob, obk = self.bank(4 + cg)
                    for j in range(HB):
                        self.mm(ob, hTv[:, j, i * 128:(i + 1) * 128], woutv[:, j, cg * 512:(cg + 1) * 512], j == 0, j == HB - 1, [hTk, woutk], [obk])
                    self.tt(oa[:, cg * 512:(cg + 1) * 512], xav[:, i, cg * 512:(cg + 1) * 512], ob, ALU.add, [xak, obk], [oak])
                g = tiles[i]
                self.store(dst[g * 128:(g + 1) * 128, :], oa, [oak], "ot%d" % (i % 2))

    def ffn_layer(self, li):
        self.ffn_phase(li, 0, self.XS, self.XS2)
        self.ffn_phase(li, 1, self.XS2, self.XS3)
        self.XS, self.XS3 = self.XS3, self.XS

    def rwkv_layer(self, li, j):
        S = self.S
        nseq = len(self.seq_lens)
        self.reset()
        nw, nwk = self.bcload("nmix", self.w["norm_mix"][li])
        xt = [self.a32(D, "x%d" % i) for i in range(2)]
        hn = [self.a16(D, "hn%d" % i) for i in range(2)]
        junk, junkk = self.a32(D, "junk")
        st, stk = self.a32(4, "st")
        z, zk = self.a16(D, "z")
        self.memset(z[0:1, :], 0.0, [zk])
        row = 0
        hbase = []
        for si, tiles in enumerate(self.seq_tiles):
            L = len(tiles) * 128
            hb0 = row + 2 * si + 1
            hbase.append(hb0)
            self.store(self.HIN[hb0 - 1:hb0, :], z[0:1, :], [zk], "r0z")
            self.store(self.HIN[hb0 + L:hb0 + L + 1, :], z[0:1, :], [zk], "r0z")
            for ti, g in enumerate(tiles):
                xa, xak = xt[g % 2]
                ha, hak = hn[g % 2]
                self.load(xa, self.XS[g * 128:(g + 1) * 128, :], [xak], "r0x%d" % (g % 2))
                self.rmsnorm_tile(xa, xak, nw, nwk, ha, hak, junk, junkk, st, stk)
                self.store(self.HIN[hb0 + ti * 128:hb0 + (ti + 1) * 128, :], ha, [hak], "r0h%d" % (g % 2))
            row += L
        if getattr(self, "stop_after", "") == "R0":
            return
        self.reset()
        wr, wrk = self.a16(3 * KC * D, "wrkv")
        wrv = wr.rearrange("p (m k c) -> p m k c", m=3, k=KC)
        for m in range(3):
            for k in range(KC):
                self.load(wrv[:, m, k, :], self.w["rw_w_rkv"][j][m][k * 128:(k + 1) * 128, :], [wrk], "w_rkv", eng="pool")
        g1, g1k = self.a16(KC * 160, "g1")
        g1v = g1.rearrange("p (k c) -> p k c", k=KC)
        for k in range(KC):
            self.load(g1v[:, k, :], self.w["rw_g1"][j][k * 128:(k + 1) * 128, :], [g1k], "w_g1", eng="pool")
        g2, g2k = self.a16(2 * D, "g2")
        g2v = g2.rearrange("p (a c) -> p a c", a=2)
        self.load(g2v[:, 0, :], self.w["rw_g2"][j][0:128, :], [g2k], "w_g2", eng="pool")
        self.load(g2v[0:32, 1, :], self.w["rw_g2"][j][128:160, :], [g2k], "w_g2", eng="pool")
        w1 = []
        a1 = []
        w2 = []
        a2 = []
        brow, browk = self.a16(D, "brow")
        self.memset(brow[0:5, :], 0.0, [browk])
        for d in range(2):
            t_, k_ = self.a16(KC * 64, "w1_%d" % d)
            for k in range(KC):
                self.load(t_[:, k * 64:(k + 1) * 64], self.w["rw_w1"][j][d][k * 128:(k + 1) * 128, :], [k_], "w_w1", eng="pool")
            w1.append((t_.rearrange("p (k c) -> p k c", k=KC), k_))
            t_, k_ = self.a16(KC * 64, "a1_%d" % d)
            for k in range(KC):
                self.load(t_[:, k * 64:(k + 1) * 64], self.w["rw_a1"][j][d][k * 128:(k + 1) * 128, :], [k_], "w_a1", eng="pool")
            a1.append((t_.rearrange("p (k c) -> p k c", k=KC), k_))
            t_, k_ = self.a16(D, "w2_%d" % d)
            self.load(t_[0:64, :], self.w["rw_w2"][j][d], [k_], "w_w2", eng="pool")
            w2.append((t_, k_))
            t_, k_ = self.a16(D, "a2_%d" % d)
            self.load(t_[0:64, :], self.w["rw_a2"][j][d], [k_], "w_a2", eng="pool")
            a2.append((t_, k_))
            self.load(brow[d:d + 1, :], self.w["rw_w0"][j][d].partition_broadcast(1), [browk], "w_b0", eng="pool")
            self.load(brow[2 + d:3 + d, :], self.w["rw_a0"][j][d].partition_broadcast(1), [browk], "w_b0", eng="pool")
        if j > 0:
            v1, v1k = self.a16(KC * 32, "v1")
            v1v = v1.rearrange("p (k c) -> p k c", k=KC)
            for k in range(KC):
                self.load(v1v[:, k, :], self.w["rw_v1"][j - 1][k * 128:(k + 1) * 128, :], [v1k], "w_v1", eng="pool")
            v2, v2k = self.a16(D, "v2")
            self.load(v2[0:32, 0:D], self.w["rw_v2"][j - 1], [v2k], "w_v2", eng="pool")
            self.load(brow[4:5, :], self.w["rw_v0"][j - 1].partition_broadcast(1), [browk], "w_b0", eng="pool")
        mu, muk = self.a32(64, "mu")
        mu48, mu48k = E1x = self.a32(128, "mu48")
        self.load(mu48[0:48, :], self.w["rw_mu"][j].rearrange("i (k p) -> (i k) p", p=128), [mu48k], "mu48")
        mb, mbk = self.bank(0)
        self.mm(mb[:, 0:48], mu48[0:48, :], self.identf[0:48, 0:48], True, True, [mu48k, "identf"], [mbk])
        self.copy(mu[:, 0:48], mb[:, 0:48], [mbk], [muk])
        kkb, kkbk = self.bcload("k_k", self.w["rw_k_k"][j])
        kab, kabk = self.bcload("k_a", self.w["rw_k_a"][j])
        rkb, rkbk = self.bcload("r_k", self.w["rw_r_k"][j])
        r32, r32k = self.a32(D, "r32")
        k32, k32k = self.a32(D, "k32")
        v32, v32k = self.a32(D, "v32")
        kk, kkk = self.a32(D, "kk")
        sgw, sgwk = self.a32(D, "sgw")
        asg, asgk = self.a32(D, "asg")
        kd, kdk = self.a32(D, "kd")
        b32, b32k = self.a32(D, "b32")
        E = [self.a32(D, "E%d" % i) for i in range(2)]
        bon, bonk = self.a32(D, "bon")
        oacc, oacck = self.a32(D, "oacc")
        st, stk = self.a32(64, "st")
        etot, etk = self.a32(32, "etot")
        Gst, Gstk = self.a16(16 * 128, "Gst")
        Gv = Gst[0:64, :].rearrange("p (h q v) -> p h q v", h=16, q=2)
        cur, curk = self.a16(D, "cur")
        sh, shk = self.a16(D, "sh")
        curT, curTk = self.a16(D, "curT")
        shT, shTk = self.a16(D, "shT")
        curTv = curT.rearrange("p (k t) -> p k t", k=KC)
        shTv = shT.rearrange("p (k t) -> p k t", k=KC)
        mixbuf = [self.a16(D, "mix%d" % i) for i in range(6)]
        mix = [(m[0][:, 0:D], m[1]) for m in mixbuf]
        mixv = [m[0].rearrange("p (k t) -> p k t", k=KC) for m in mix]
        sm, smk = self.a16(2 * 128, "smallT")
        V, Vk = mix[3]
        rt_, rtk = cur, curk
        bt_, btk = sh, shk
        kt_, ktk = curT, curTk
        at_, atk = shT, shTk
        Bh, Bhk = mix[0]
        Kh, Khk = mix[2]
        BhX = mixbuf[0][0]
        KhX = mixbuf[2][0]
        XX = mixbuf[5][0]
        arT, arTk = self.a16(16 * 256, "arT")
        arTv = arT[0:64, :].rearrange("p (h a t) -> p h a t", h=16, a=2)
        bT, bTk = self.a16(16 * 128, "bT")
        bTv = bT[0:64, :].rearrange("p (h t) -> p h t", h=16)
        kT, kTk = self.a16(16 * 128, "kT")
        kTv = kT[0:64, :].rearrange("p (h t) -> p h t", h=16)
        SC, SCk = self.a16(4 * 512, "SC")
        SCv = SC.rearrange("p (h c) -> p h c", h=4)
        Mp = [None] + [self.a16(512, "Mp%d" % i) for i in range(1, 6)]
        NX, NXk = self.a16(1536, "NX")
        Np2 = [(NX[:, 0:512], NXk), (NX[:, 512:1024], NXk)]
        Np = [Np2[i % 2] for i in range(6)]
        X2v = NX.rearrange("p (h q c) -> p h q c", h=4, q=2)
        X = [(mix[5][0][:, 0:512], mix[5][1]), (mix[5][0][:, 512:1024], mix[5][1])]
        PTs, PTsk = self.a16(16 * 128, "PTs")
        PTv = PTs[0:64, :].rearrange("p (h q k) -> p h q k", h=16, q=2)
        RTs, RTsk = self.a16(16 * 128, "RTs")
        RTv = RTs[0:64, :].rearrange("p (h t) -> p h t", h=16)
        Hbf, Hbfk = self.a16(D, "Hbf")
        Hv = Hbf[0:64, :].rearrange("p (h v) -> p h v", h=16)
        rot = [0]

        def nb():
            rot[0] = (rot[0] + 1) % 4
            return self.bank(rot[0])

        def lin2(lhs_chunks, rhs_fn, evac):
            for cg in range(2):
                pb, pk = nb()
                n = len(lhs_chunks)
                for i, (l, lk) in enumerate(lhs_chunks):
                    r, rk_ = rhs_fn(i, cg)
                    self.mm(pb, l, r, i == 0, i == n - 1, [lk, rk_], [pk])
                evac(cg, pb, pk)

        def recur(d, q_order, hg, oa, oak):
            ib, ibk = self.bank(5)
            for qi, q in enumerate(q_order):
                hb_, hbk_ = nb()
                for hh in range(4):
                    h = hg * 4 + hh
                    self.mm(ib[:, q * 256 + hh * 64:q * 256 + (hh + 1) * 64], RTv[:, h, :], Hv[:, h, :], True, True, [RTsk, Hbfk + str(hg)], [ibk])
                    self.mm(hb_[0:64, hh * 64:(hh + 1) * 64], PTv[:, h, q, :], Hv[:, h, :], True, True, [PTsk, Hbfk + str(hg)], [hbk_])
                self.tt(Hv[:, hg * 4:(hg + 1) * 4, :], hb_[0:64, 0:256].rearrange("p (h v) -> p h v", h=4), Gv[:, hg * 4:(hg + 1) * 4, q, :], ALU.add,
                        [hbk_, Gstk], [Hbfk + str(hg)])
            cs = slice(hg * 256, (hg + 1) * 256)
            for q in range(2):
                ps = slice(64 * q, 64 * q + 64)
                self.tt(oa[ps, cs], oa[ps, cs], ib[ps, q * 256:(q + 1) * 256], ALU.add, [ibk, oak], [oak])

        def zero_H():
            self.memset(Hbf[0:64, :], 0.0, [Hbfk + str(hg) for hg in range(4)])

        for si, tiles in enumerate(self.seq_tiles):
            zero_H()
            for ti, g in enumerate(tiles):
                try:
                    r0 = g * 128
                    hb = hbase[si] + ti * 128
                    self.load(cur, self.HIN[hb:hb + 128, :], [curk], "r1cur")
                    self.load(sh[:, 0:512], self.HIN[hb - 1:hb + 127, 0:512], [shk], "r1sh")
                    self.load(sh[:, 512:D], self.HIN[hb + 1:hb + 129, 512:D], [shk], "r1sh")
                    self.transpose8(cur, curk, curT, curTk, 0)
                    self.transpose8(sh, shk, shT, shTk, 1)
                    self.tt(shT, shT, curT, ALU.subtract, [shTk, curTk], [shTk])
                    for i in range(6):
                        for k in range(KC):
                            self.stt(mixv[i][:, k, :], shTv[:, k, :], mu[:, i * 8 + k:i * 8 + k + 1], curTv[:, k, :], ALU.mult, ALU.add,
                                     [shTk, muk, curTk], [mix[i][1]])
                    self.cutpoint(1)
                    for m, (mi, dst, dk) in ((1, (2, k32, k32k)), (2, (3, v32, v32k)), (0, (0, r32, r32k))):
                        lin2([(mixv[mi][:, k, :], mix[mi][1]) for k in range(KC)], lambda i, cg, m=m: (wrv[:, m, i, cg * 512:(cg + 1) * 512], wrk),
                             lambda cg, pb, pk, dst=dst, dk=dk, m=m: self.copy(dst[:, cg * 512:(cg + 1) * 512], pb, [pk], [dk], eng=("act" if m != 1 else "dve")))
                    if g == 0:
                        self.dump("r32a", r32, r32k, [128, D])
                    if j == 0:
                        self.store(self.VFIRST[r0:r0 + 128, :], v32, [v32k], "r1vf")
                    else:
                        vb_, vbk_ = nb()
                        for k in range(KC):
                            self.mm(vb_[0:32, 0:128], v1v[:, k, :], mixv[3][:, k, :], k == 0, k == KC - 1, [v1k, mix[3][1]], [vbk_])
                        self.copy(sm[0:32, 0:128], vb_[0:32, 0:128], [vbk_], [smk])
                        self.load(E[0][0], self.VFIRST[r0:r0 + 128, :], [E[0][1]], "r1vfl")
                        lin2([(sm[0:32, 0:128], smk), (self.sel5[0:5, 512:640], "sel5")],
                             lambda i, cg: ((v2[0:32, cg * 512:(cg + 1) * 512], v2k) if i == 0 else (brow[0:5, cg * 512:(cg + 1) * 512], browk)),
                             lambda cg, pb, pk: self.act(E[1][0][:, cg * 512:(cg + 1) * 512], pb, AF.Sigmoid, [pk], [E[1][1]]))
                        self.tt(E[0][0], E[0][0], v32, ALU.subtract, [E[0][1], v32k], [E[0][1]])
                        self.tt(E[0][0], E[0][0], E[1][0], ALU.mult, [E[0][1], E[1][1]], [E[0][1]])
                        self.tt(v32, v32, E[0][0], ALU.add, [v32k, E[0][1]], [v32k])
                    self.copy(V, v32, [v32k], [Vk], eng="act")
                    if g == 0:
                        self.dump("r32b", r32, r32k, [128, D])
                    for mg, (m0, m1) in enumerate(((0, 128), (128, 160))):
                        gb_, gbk_ = nb()
                        for k in range(KC):
                            self.mm(gb_[0:m1 - m0, 0:128], g1v[:, k, m0:m1], mixv[5][:, k, :], k == 0, k == KC - 1, [g1k, mix[5][1]], [gbk_])
                        self.act(sm[0:m1 - m0, mg * 128:(mg + 1) * 128], gb_[0:m1 - m0, 0:128], AF.Sigmoid, [gbk_], [smk + "g"])
                    lin2([(sm[:, 0:128], smk + "g"), (sm[0:32, 128:256], smk + "g")],
                         lambda i, cg: (g2v[:, 0, cg * 512:(cg + 1) * 512], g2k) if i == 0 else (g2v[0:32, 1, cg * 512:(cg + 1) * 512], g2k),
                         lambda cg, pb, pk: self.copy(E[1][0][:, cg * 512:(cg + 1) * 512], pb, [pk], [E[1][1]], eng="act"))
                    self.store(self.GATE[r0:r0 + 128, :], E[1][0], [E[1][1]], "r1gate")
                    self.cutpoint(2)
                    if g == 0:
                        self.dump("r32", r32, r32k, [128, D])
                        self.dump("k32", k32, k32k, [128, D])
                        self.dump("v32", v32, v32k, [128, D])
                        self.dump("gate", E[1][0], E[1][1], [128, D])
                        self.dump("mix1", mix[1][0], mix[1][1], [128, D], BF16)
                        self.dump("mix0", mix[0][0], mix[0][1], [128, D], BF16)
                        self.dump("mu", mu[:, 0:48], muk, [128, 48])
                        self.dump("wr0", wr[:, 0:D], wrk, [128, D], BF16)
                        self.dump("wrall", wr[:, 0:8 * D], wrk, [128, 8 * D], BF16)
                        self.dump("wk0", wr[:, 8 * D:9 * D], wrk, [128, D], BF16)
                        self.dump("curT", curT, curTk, [128, D], BF16)
                        self.dump("kkb", kkb, kkbk, [128, D])
                    self.tt(kk, k32, kkb, ALU.mult, [k32k, kkbk], [kkk])
                    self.act(E[0][0], kk, AF.Square, [kkk], [E[0][1]])
                    self.reduce(st[:, 0:16], E[0][0].rearrange("p (h n) -> p h n", h=16), [E[0][1]], [stk])
                    self.act(st[:, 16:32], st[:, 0:16], AF.Sqrt, [stk], [stk])
                    self.ts(st[:, 16:32], st[:, 16:32], 1e-12, None, ALU.max, None, [stk], [stk])
                    self.recip(st[:, 32:48], st[:, 16:32], [stk], [stk])
                    self.tt(kk.rearrange("p (h n) -> p h n", h=16), kk.rearrange("p (h n) -> p h n", h=16),
                            st[:, 32:48].unsqueeze(2).to_broadcast([128, 16, 64]), ALU.mult, [kkk, stk], [kkk])
                    self.cutpoint(3)
                    for d in range(2):
                        wb_, wbk_ = nb()
                        for k in range(KC):
                            self.mm(wb_[0:64, 0:128], w1[d][0][:, k, :], mixv[1][:, k, :], k == 0, k == KC - 1, [w1[d][1], mix[1][1]], [wbk_])
                        self.act(sm[0:64, 0:128], wb_[0:64, 0:128], AF.Tanh, [wbk_], [smk])
                        lin2([(sm[0:64, 0:128], smk), (self.sel5[0:5, d * 128:(d + 1) * 128], "sel5")],
                             lambda i, cg, d=d: ((w2[d][0][0:64, cg * 512:(cg + 1) * 512], w2[d][1]) if i == 0 else (brow[0:5, cg * 512:(cg + 1) * 512], browk)),
                             lambda cg, pb, pk: self.act(sgw[:, cg * 512:(cg + 1) * 512], pb, AF.Sigmoid, [pk], [sgwk]))
                        ab_, abk_ = nb()
                        for k in range(KC):
                            self.mm(ab_[0:64, 0:128], a1[d][0][:, k, :], mixv[4][:, k, :], k == 0, k == KC - 1, [a1[d][1], mix[4][1]], [abk_])
                        self.copy(sm[0:64, 128:256], ab_[0:64, 0:128], [abk_], [smk + "a"], eng="act")
                        lin2([(sm[0:64, 128:256], smk + "a"), (self.sel5[0:5, (2 + d) * 128:(3 + d) * 128], "sel5")],
                             lambda i, cg, d=d: ((a2[d][0][0:64, cg * 512:(cg + 1) * 512], a2[d][1]) if i == 0 else (brow[0:5, cg * 512:(cg + 1) * 512], browk)),
                             lambda cg, pb, pk: self.act(asg[:, cg * 512:(cg + 1) * 512], pb, AF.Sigmoid, [pk], [asgk]))
                        self.stt(kd, asg, -1.0, kab, ALU.add, ALU.mult, [asgk, kabk], [kdk])
                        self.stt(kd, kd, 1.0, k32, ALU.add, ALU.mult, [kdk, k32k], [kdk])
                        self.tt(b32, kk, asg, ALU.mult, [kkk, asgk], [b32k])
                        self.tt(E[0][0], r32, rkb, ALU.mult, [r32k, rkbk], [E[0][1]])
                        self.tt(E[0][0], E[0][0], kd, ALU.mult, [E[0][1], kdk], [E[0][1]])
                        self.reduce(st[:, 48:64], E[0][0].rearrange("p (h n) -> p h n", h=16), [E[0][1]], [stk])
                        if d == 0:
                            self.tt(bon.rearrange("p (h n) -> p h n", h=16), v32.rearrange("p (h n) -> p h n", h=16),
                                    st[:, 48:64].unsqueeze(2).to_broadcast([128, 16, 64]), ALU.mult, [v32k, stk], [bonk])
                        else:
                            self.tt(E[0][0].rearrange("p (h n) -> p h n", h=16), v32.rearrange("p (h n) -> p h n", h=16),
                                    st[:, 48:64].unsqueeze(2).to_broadcast([128, 16, 64]), ALU.mult, [v32k, stk], [E[0][1]])
                            self.tt(bon, bon, E[0][0], ALU.add, [bonk, E[0][1]], [bonk])
                        tb, tbk = self.bank(4)
                        for h in range(16):
                            self.mm(tb[0:64, h * 2:h * 2 + 2], sgw[:, h * 64:(h + 1) * 64], self.cs["rcsel"][:], True, True, [sgwk, "cs_rcsel"], [tbk])
                        self.act(etot[0:64, :], tb[0:64, 0:32], AF.Exp, [tbk], [etk])
                        self.cutpoint(4)
                        if g == 0 and d == 0:
                            self.dump("kk", kk, kkk, [128, D])
                            self.dump("st", st, stk, [128, 64])
                            self.dump("sgw", sgw, sgwk, [128, D])
                            self.dump("asg", asg, asgk, [128, D])
                            self.dump("kd", kd, kdk, [128, D])
                            self.dump("etot", etot[0:64, :], etk, [64, 32])
                        def cums(nm):
                            bs = [nb(), nb()]
                            for c in range(2):
                                self.mm(bs[c][0], self.cs["r%s_%d" % (nm, d)][:], sgw[:, c * 512:(c + 1) * 512], True, True, ["cs_r%s_%d" % (nm, d), sgwk], [bs[c][1]])
                            return bs
                        gbs = cums("linc")
                        for c in range(2):
                            cs = slice(c * 512, (c + 1) * 512)
                            self.act(E[0][0][:, cs], gbs[c][0], AF.Exp, [gbs[c][1]], [E[0][1]])
                            self.act(E[1][0][:, cs], gbs[c][0], AF.Exp, [gbs[c][1]], [E[1][1]], scale=-1.0)
                        self.tt(rt_, r32, E[0][0], ALU.mult, [r32k, E[0][1]], [rtk])
                        self.tt(bt_, b32, E[1][0], ALU.mult, [b32k, E[1][1]], [btk])
                        self.tt(kt_, kd, E[1][0], ALU.mult, [kdk, E[1][1]], [ktk])
                        xbs = cums("lexc")
                        rbs = cums("lrem")
                        for c in range(2):
                            cs = slice(c * 512, (c + 1) * 512)
                            self.act(E[0][0][:, cs], xbs[c][0], AF.Exp, [xbs[c][1]], [E[0][1]])
                            self.act(E[1][0][:, cs], rbs[c][0], AF.Exp, [rbs[c][1]], [E[1][1]])
                        self.stt(at_, kk, -1.0, E[0][0], ALU.mult, ALU.mult, [kkk, E[0][1]], [atk])
                        self.tt(Bh, b32, E[1][0], ALU.mult, [b32k, E[1][1]], [Bhk])
                        self.tt(Kh, kd, E[1][0], ALU.mult, [kdk, E[1][1]], [Khk])
                        self.cutpoint(5)
                        if g == 0:
                            self.dump("at%d" % d, at_, atk, [128, D], BF16)
                            self.dump("rt%d" % d, rt_, rtk, [128, D], BF16)
                            self.dump("bt%d" % d, bt_, btk, [128, D], BF16)
                            self.dump("kt%d" % d, kt_, ktk, [128, D], BF16)
                            self.dump("Bh%d" % d, Bh, Bhk, [128, D], BF16)
                            self.dump("Kh%d" % d, Kh, Khk, [128, D], BF16)
                        for (src, srck, dstf, dstk) in ((at_, atk, lambda h0: arTv[:, h0:h0 + 8, 0, :], arTk), (rt_, rtk, lambda h0: arTv[:, h0:h0 + 8, 1, :], arTk),
                                                       (bt_, btk, lambda h0: bTv[:, h0:h0 + 8, :], bTk), (kt_, ktk, lambda h0: kTv[:, h0:h0 + 8, :], kTk)):
                            for half in range(2):
                                pb, pk = self.tbank(half)
                                for hh in range(8):
                                    h = half * 8 + hh
                                    self.tr(pb[0:64, hh * 128:(hh + 1) * 128], src[:, h * 64:(h + 1) * 64], [srck], [pk])
                                self.copy(dstf(half * 8), pb[0:64, :].rearrange("p (h t) -> p h t", h=8), [pk], [dstk], eng=("act" if half == 0 else "dve"))
                        self.cutpoint(6)
                        for hg in range(4):
                            for hh in range(4):
                                h = hg * 4 + hh
                                sb, sbk = self.bank(hh)
                                self.mm(sb[:, 0:256], bTv[:, h, :], arTv[:, h, :, :], True, True, [bTk, arTk], [sbk])
                                self.mm(sb[:, 256:512], kTv[:, h, :], arTv[:, h, :, :], True, True, [kTk, arTk], [sbk])
                            n4, n4k = self.bank(4)
                            for hh in range(4):
                                h = hg * 4 + hh
                                self.mm(n4[:, hh * 128:(hh + 1) * 128], arTv[:, h, 0, :], bTv[:, h, :], True, True, [arTk, bTk], [n4k])
                            for hh in range(4):
                                sb, sbk = self.bank(hh)
                                self.tt(SCv[:, hh, :], sb, self.cs["mask4_%d" % d][:], ALU.mult, [sbk, "cs_mask4_%d" % d], [SCk])
                            self.tt(Np[0][0].rearrange("p (h t) -> p h t", h=4), n4.rearrange("p (h t) -> p h t", h=4),
                                    self.cs["maskn_%d" % d][:].unsqueeze(1).to_broadcast([128, 4, 128]), ALU.mult, [n4k, "cs_maskn_%d" % d], [Np[0][1]])
                            self.cutpoint(7)
                            def Mj(jj, hh):
                                return (SCv[:, hh, 0:128], SCk) if jj == 0 else (Mp[jj][0][:, hh * 128:(hh + 1) * 128], Mp[jj][1])
                            self.cutpoint(8)
                            xb, xbk = self.bank(4)
                            for hh in range(4):
                                h = hg * 4 + hh
                                self.mm(xb[:, hh * 128:hh * 128 + 64], SCv[:, hh, 256:384], V[:, h * 64:(h + 1) * 64], True, True, [SCk, Vk], [xbk])
                            Xc, Xck = X[0]
                            Xcv = Xc.rearrange("p (h c) -> p h c", h=4)
                            self.copy(Xcv[:, :, 0:64], xb.rearrange("p (h c) -> p h c", h=4)[:, :, 0:64], [xbk], [Xck], eng="act")
                            self.copy(Xcv[:, :, 64:128], at_[:, hg * 256:(hg + 1) * 256].rearrange("p (h c) -> p h c", h=4), [atk], [Xck], eng="dve")
                            for jj in range(6):
                                if jj < 5:
                                    ba, bak = self.bank(jj % 2)
                                    bb, bbk = self.bank(2 + jj % 2)
                                    for hh in range(4):
                                        m_, mk_ = Mj(jj, hh)
                                        n_ = Np[jj][0][:, hh * 128:(hh + 1) * 128]
                                        self.mm(ba[:, hh * 128:(hh + 1) * 128], n_, m_, True, True, [Np[jj][1], mk_], [bak])
                                        if jj < 4:
                                            self.mm(bb[:, hh * 128:(hh + 1) * 128], m_, n_, True, True, [Np[jj][1], mk_], [bbk])
                                xb, xbk = self.bank(4 + (jj + 1) % 2)
                                Xc, Xck = X[jj % 2]
                                Xn, Xnk = X[(jj + 1) % 2]
                                for hh in range(4):
                                    m_, mk_ = Mj(jj, hh)
                                    self.mm(xb[:, hh * 128:(hh + 1) * 128], m_, Xc[:, hh * 128:(hh + 1) * 128], True, True, [mk_, Xck], [xbk])
                                if jj < 5:
                                    self.copy(Mp[jj + 1][0], ba, [bak], [Mp[jj + 1][1]], eng="act")
                                    if jj < 4:
                                        self.copy(Np[jj + 1][0], bb, [bbk], [Np[jj + 1][1]], eng="dve")
                                self.tt(Xn, Xc, xb, ALU.add, [Xck, xbk], [Xnk])
                            Xf, Xfk = X[0]
                            Xfv = Xf.rearrange("p (h c) -> p h c", h=4)
                            self.cutpoint(9)
                            for q in range(2):
                                self.ts(X2v[:, :, q, 0:128], Xfv, self.cs["csel"][:, q:q + 1], None, ALU.mult, None, [Xfk, "cs_csel"], [NXk])
                                self.ts(X2v[:, :, q, 128:192], V[:, hg * 256:(hg + 1) * 256].rearrange("p (h c) -> p h c", h=4), self.cs["csel"][:, q:q + 1], None,
                                        ALU.mult, None, [Vk, "cs_csel"], [NXk])
                            gb_, gbk_ = self.bank(0)
                            pb_, pbk_ = self.bank(1)
                            for hh in range(4):
                                h = hg * 4 + hh
                                hc = slice(h * 64, (h + 1) * 64)
                                self.mm(gb_[0:64, hh * 128:(hh + 1) * 128], Bh[:, hc], X2v[:, hh, :, 0:64], True, False, [Bhk, NXk], [gbk_])
                                self.mm(gb_[0:64, hh * 128:(hh + 1) * 128], Kh[:, hc], X2v[:, hh, :, 128:192], False, True, [Khk, NXk], [gbk_])
                                for q in range(2):
                                    oc = slice((hh * 2 + q) * 64, (hh * 2 + q + 1) * 64)
                                    self.mm(pb_[0:64, oc], X2v[:, hh, q, 64:128], Bh[:, hc], True, True, [Bhk, NXk], [pbk_])
                            self.cutpoint(91)
                            self.copy(Gst[0:64, hg * 512:(hg + 1) * 512], gb_[0:64, :], [gbk_], [Gstk], eng="act")
                            self.cutpoint(92)
                            for hh in range(4):
                                h = hg * 4 + hh
                                for q in range(2):
                                    oc = slice((hh * 2 + q) * 64, (hh * 2 + q + 1) * 64)
                                    self.stt(PTv[0:64, h, q, :], self.identf[0:64, 0:64], etot[0:64, h * 2 + q:h * 2 + q + 1], pb_[0:64, oc], ALU.mult, ALU.add,
                                             ["identf", etk, pbk_], [PTsk])
                            self.cutpoint(10)
                            rb_, rbk_ = self.bank(2)
                            ob_, obk_ = self.bank(3)
                            for hh in range(4):
                                h = hg * 4 + hh
                                hc = slice(h * 64, (h + 1) * 64)
                                self.mm(rb_[0:64, hh * 128:(hh + 1) * 128], Xfv[:, hh, 64:128], SCv[:, hh, 128:256], True, False, [Xfk, SCk], [rbk_])
                                self.mm(rb_[0:64, hh * 128:(hh + 1) * 128], rt_[:, hc], self.identb[:], False, True, [rtk, "identb"], [rbk_])
                                self.mm(ob_[:, hh * 64:(hh + 1) * 64], SCv[:, hh, 128:256], Xfv[:, hh, 0:64], True, False, [SCk, Xfk], [obk_])
                                self.mm(ob_[:, hh * 64:(hh + 1) * 64], SCv[:, hh, 384:512], V[:, hc], False, True, [SCk, Vk], [obk_])
                            self.copy(RTs[0:64, hg * 512:(hg + 1) * 512], rb_[0:64, :], [rbk_], [RTsk], eng="act")
                            cs = slice(hg * 256, (hg + 1) * 256)
                            if d == 0:
                                self.copy(oacc[:, cs], ob_[:, 0:256], [obk_], [oacck], eng="dve")
                                recur(0, (0, 1), hg, oacc, oacck)
                            else:
                                self.tt(oacc[:, cs], oacc[:, cs], ob_[:, 0:256], ALU.add, [obk_, oacck], [oacck])
                        if d == 1:
                            self.store(self.PT[g], PTs[0:64, :], [PTsk], "r1pt")
                            self.store(self.GG[g], Gst[0:64, :], [Gstk], "r1gg")
                            self.store(self.RT[g], RTs[0:64, :], [RTsk], "r1rt")
                    if g == 0:
                        self.dump("oacc_r1", oacc, oacck, [128, D])
                        self.dump("bon", bon, bonk, [128, D])
                    self.store(self.OACC[r0:r0 + 128, :], oacc, [oacck], "r1oacc")
                    self.store(self.BONUS[r0:r0 + 128, :], bon, [bonk], "r1bon")
                except _Cut:
                    pass
        if getattr(self, "stop_after", "") == "R1":
            return
        self.reset()
        wo, wok = self.a16(KC * D, "rwo")
        wov = wo.rearrange("p (k c) -> p k c", k=KC)
        for k in range(KC):
            self.load(wov[:, k, :], self.w["rw_w_o"][j][k * 128:(k + 1) * 128, :], [wok], "w_rwo", eng="pool")
        lw_, lwk = self.bcload("lnx_w", self.w["rw_lnx_w"][j])
        lb_, lbk = self.bcload("lnx_b", self.w["rw_lnx_b"][j])
        x32, xk = self.a32(D, "x")
        junk, junkk = self.a32(D, "junk")
        st, stk = self.a32(96, "st")
        gt, gtk = self.a32(D, "gate")
        bon, bonk = self.a32(D, "bon")
        oacc, oacck = self.a32(D, "oacc")
        xo, xok = self.a32(D, "xo")
        Gst, Gstk = self.a16(16 * 128, "Gst")
        Gv = Gst[0:64, :].rearrange("p (h q v) -> p h q v", h=16, q=2)
        PTs, PTsk = self.a16(16 * 128, "PTs")
        PTv = PTs[0:64, :].rearrange("p (h q k) -> p h q k", h=16, q=2)
        RTs, RTsk = self.a16(16 * 128, "RTs")
        RTv = RTs[0:64, :].rearrange("p (h t) -> p h t", h=16)
        Hbf, Hbfk = self.a16(D, "Hbf")
        Hv = Hbf[0:64, :].rearrange("p (h v) -> p h v", h=16)
        on, onk = self.a16(D, "on")
        onT, onTk = self.a16(D, "onT")
        onTv = onT.rearrange("p (k t) -> p k t", k=KC)
        for si, tiles in enumerate(self.seq_tiles):
            zero_H()
            for g in reversed(tiles):
                r0 = g * 128
                self.load(PTs[0:64, :], self.PT[g], [PTsk], "r2pt")
                self.load(Gst[0:64, :], self.GG[g], [Gstk], "r2gg")
                self.load(RTs[0:64, :], self.RT[g], [RTsk], "r2rt")
                self.load(oacc, self.OACC[r0:r0 + 128, :], [oacck], "r2oa")
                self.load(bon, self.BONUS[r0:r0 + 128, :], [bonk], "r2bon")
                self.load(gt, self.GATE[r0:r0 + 128, :], [gtk], "r2g")
                self.load(x32, self.XS[r0:r0 + 128, :], [xk], "r2x")
                for hg in range(4):
                    recur(1, (1, 0), hg, oacc, oacck)
                if g == 0:
                    self.dump("oacc_r2", oacc, oacck, [128, D])
                o3 = oacc.rearrange("p (h n) -> p h n", h=16)
                self.reduce(st[:, 0:16], o3, [oacck], [stk])
                self.act(junk, oacc, AF.Square, [oacck], [junkk])
                self.reduce(st[:, 16:32], junk.rearrange("p (h n) -> p h n", h=16), [junkk], [stk])
                self.ts(st[:, 32:48], st[:, 0:16], 1.0 / 64, None, ALU.mult, None, [stk], [stk])
                self.tt(st[:, 48:64], st[:, 32:48], st[:, 32:48], ALU.mult, [stk], [stk])
                self.stt(st[:, 64:80], st[:, 16:32], 1.0 / 64, st[:, 48:64], ALU.mult, ALU.subtract, [stk], [stk])
                self.ts(st[:, 64:80], st[:, 64:80], RW_LN_EPS, None, ALU.add, None, [stk], [stk])
                self.act(st[:, 64:80], st[:, 64:80], AF.Sqrt, [stk], [stk])
                self.recip(st[:, 80:96], st[:, 64:80], [stk], [stk])
                self.tt(o3, o3, st[:, 32:48].unsqueeze(2).to_broadcast([128, 16, 64]), ALU.subtract, [oacck, stk], [oacck])
                self.tt(o3, o3, st[:, 80:96].unsqueeze(2).to_broadcast([128, 16, 64]), ALU.mult, [oacck, stk], [oacck])
                self.tt(oacc, oacc, lw_, ALU.mult, [oacck, lwk], [oacck])
                self.tt(oacc, oacc, lb_, ALU.add, [oacck, lbk], [oacck])
                self.tt(oacc, oacc, bon, ALU.add, [oacck, bonk], [oacck])
                self.tt(on, oacc, gt, ALU.mult, [oacck, gtk], [onk])
                self.transpose8(on, onk, onT, onTk, 0)
                for cg in range(2):
                    pb, pk = nb()
                    for k in range(KC):
                        self.mm(pb, onTv[:, k, :], wov[:, k, cg * 512:(cg + 1) * 512], k == 0, k == KC - 1, [onTk, wok], [pk])
                    self.tt(xo[:, cg * 512:(cg + 1) * 512], x32[:, cg * 512:(cg + 1) * 512], pb, ALU.add, [xk, pk], [xok])
                self.store(self.XS[r0:r0 + 128, :], xo, [xok], "r2xo")

    def lb_phase(self):
        self.reset()
        nc = self.nc
        self.LBS = nc.dram_tensor("LBS", [4, D], F32).ap()
        lg, lgk = self.a32(4 * D, "lg")
        lgv = lg.rearrange("p (l c) -> p l c", l=4)
        self.load(lg[0:1, :], self.w["hg_lb_logits"].rearrange("l c -> (l c)").partition_broadcast(1), [lgk], "lg")
        e, ek = self.a32(4 * D, "le")
        ev = e.rearrange("p (l c) -> p l c", l=4)
        sm, smk = self.a32(D, "lsum")
        o, ok = self.a32(4 * D, "lo")
        ov = o.rearrange("p (l c) -> p l c", l=4)
        self.act(e[0:1, :], lg[0:1, :], AF.Exp, [lgk], [ek])
        self.tt(sm[0:1, :], ev[0:1, 0, :], ev[0:1, 1, :], ALU.add, [ek], [smk])
        self.tt(sm[0:1, :], sm[0:1, :], ev[0:1, 2, :], ALU.add, [ek, smk], [smk])
        self.tt(sm[0:1, :], sm[0:1, :], ev[0:1, 3, :], ALU.add, [ek, smk], [smk])
        self.recip(sm[0:1, :], sm[0:1, :], [smk], [smk])
        self.tt(ov[0:1, 0, :], ev[0:1, 1, :], sm[0:1, :], ALU.mult, [ek, smk], [ok])
        self.tt(ov[0:1, 2, :], ev[0:1, 1, :], ev[0:1, 2, :], ALU.add, [ek], [ok])
        self.tt(ov[0:1, 2, :], ov[0:1, 2, :], ev[0:1, 3, :], ALU.add, [ek, ok], [ok])
        self.tt(ov[0:1, 2, :], ov[0:1, 2, :], sm[0:1, :], ALU.mult, [smk, ok], [ok])
        self.ts(ov[0:1, 1, :], ov[0:1, 0, :], -1.0, 1.0, ALU.mult, ALU.add, [ok], [ok])
        self.ts(ov[0:1, 3, :], ov[0:1, 2, :], -1.0, 1.0, ALU.mult, ALU.add, [ok], [ok])
        self.store(self.LBS.rearrange("l c -> (l c)").partition_broadcast(1), o[0:1, :], [ok], "lbs")

    def hgrn_layer(self, li, j):
        if not hasattr(self, "LBS"):
            self.lb_phase()
        S = self.S
        self.reset()
        win, wink = self.a16(KC * 5 * D, "hwin")
        winv = win.rearrange("p (k c) -> p k c", k=KC)
        for k in range(KC):
            for c5 in range(5):
                self.load(winv[:, k, c5 * D:(c5 + 1) * D], self.w["hg_w_in"][j][k * 128:(k + 1) * 128, c5 * D:(c5 + 1) * D], [wink], "w_hwin", eng="pool")
        nw, nwk = self.bcload("nmix", self.w["norm_mix"][li])
        lb, lbk = self.bcload("lb", self.LBS[2 * j])
        om, omk = self.bcload("omlb", self.LBS[2 * j + 1])
        x32, xk = self.a32(D, "x")
        junk, junkk = self.a32(D, "junk")
        st, stk = self.a32(4, "st")
        qs, qsk = self.a32(D, "qs")
        sgt, sgtk = self.a32(D, "sgate")
        sig = [self.a32(D, "sig%d" % d) for d in range(2)]
        kf, kfk = self.a32(D, "kf")
        logf, logfk = self.a32(D, "logf")
        E = [self.a32(D, "E%d" % i) for i in range(2)]
        S32, S32k = self.a32(D, "S32")
        oacc, oacck = self.a32(D, "oacc")
        etot = [self.a32(16, "etot%d" % d) for d in range(2)]
        xn, xnk = self.a16(D, "xn")
        xnT, xnTk = self.a16(D, "xnT")
        xnTv = xnT.rearrange("p (k t) -> p k t", k=KC)
        V, Vk = self.a16(D, "V")
        qt = [self.a16(D, "qt%d" % d) for d in range(2)]
        kt = [self.a16(D, "kt%d" % d) for d in range(2)]
        kh = [self.a16(D, "kh%d" % d) for d in range(2)]
        qT = [self.a16(D, "qT%d" % d) for d in range(2)]
        kT = [self.a16(D, "kT%d" % d) for d in range(2)]
        AT, ATk = self.a16(D, "AT")
        ATv = AT.rearrange("p (h t) -> p h t", h=8)
        Sbf, Sbfk = self.a16(D, "Sbf")
        rot = [0]

        def nb():
            rot[0] = (rot[0] + 1) % 4
            return self.bank(rot[0])

        def recur_tile(d, qTa, qTk, kha, khk, Va, Vka, eta, etk, first_chunk_of_seq, want_intra, ATv_, ATk_, acc_mode):
            qTv = qTa.rearrange("p (h t) -> p h t", h=8)
            order = (0, 1) if d == 0 else (1, 0)
            for hg in range(2):
                ob, obk = self.bank(4 + hg)
                for hh in range(4):
                    h = hg * 4 + hh
                    osl = ob[:, hh * 128:(hh + 1) * 128]
                    if want_intra:
                        self.mm(osl, ATv_[:, h, :], Va[:, h * 128:(h + 1) * 128], True, False, [ATk_, Vka], [obk])
                    for qi, q in enumerate(order):
                        ps = slice(64 * q, 64 * q + 64)
                        self.mm(ob[ps, hh * 128:(hh + 1) * 128], qTv[:, h, 64 * q:64 * q + 64], Sbf[:, h * 128:(h + 1) * 128],
                                not want_intra, qi == 1, [qTk, Sbfk + str(h)], [obk])
                        kvb, kvk = nb()
                        self.mm(kvb[:, 0:128], kha[ps, h * 128:(h + 1) * 128], Va[ps, h * 128:(h + 1) * 128], True, True, [khk, Vka], [kvk])
                        self.stt(S32[:, h * 128:(h + 1) * 128], S32[:, h * 128:(h + 1) * 128], eta[:, h * 2 + q:h * 2 + q + 1], kvb[:, 0:128],
                                 ALU.mult, ALU.add, [S32k + str(h), etk, kvk], [S32k + str(h)])
                        self.copy(Sbf[:, h * 128:(h + 1) * 128], S32[:, h * 128:(h + 1) * 128], [S32k + str(h)], [Sbfk + str(h)], eng="act")
                cs = slice(hg * 512, (hg + 1) * 512)
                if acc_mode == "set":
                    self.copy(oacc[:, cs], ob, [obk], [oacck], eng="act")
                else:
                    self.tt(oacc[:, cs], oacc[:, cs], ob, ALU.add, [obk, oacck], [oacck])

        def zero_state():
            self.memset(S32, 0.0, [S32k + str(h) for h in range(8)])
            self.memset(Sbf, 0.0, [Sbfk + str(h) for h in range(8)])

        for tiles in self.seq_tiles:
            zero_state()
            for g in tiles:
                r0 = g * 128
                self.load(x32, self.XS[r0:r0 + 128, :], [xk], "hx")
                self.rmsnorm_tile(x32, xk, nw, nwk, xn, xnk, junk, junkk, st, stk)
                self.transpose8(xn, xnk, xnT, xnTk, 0)
                for c in range(10):
                    pb, pk = nb()
                    for k in range(KC):
                        self.mm(pb, xnTv[:, k, :], winv[:, k, c * 512:(c + 1) * 512], k == 0, k == KC - 1, [xnTk, wink], [pk])
                    cs = slice((c % 2) * 512, (c % 2) * 512 + 512)
                    if c < 2:
                        self.act(qs[:, cs], pb, AF.Silu, [pk], [qsk])
                    elif c < 4:
                        self.copy(V[:, cs], pb, [pk], [Vk], eng="dve")
                    elif c < 6:
                        self.act(sgt[:, cs], pb, AF.Silu, [pk], [sgtk])
                    else:
                        sa, sk_ = sig[(c - 6) // 2]
                        self.act(sa[:, cs], pb, AF.Sigmoid, [pk], [sk_])
                self.store(self.GATE[r0:r0 + 128, :], sgt, [sgtk], "hgate")
                self.store(self.VV[r0:r0 + 128, :], V, [Vk], "hvv")
                for d in range(2):
                    sa, sk_ = sig[d]
                    self.tt(sa, sa, om, ALU.mult, [sk_, omk], [sk_])
                    self.tt(sa, sa, lb, ALU.add, [sk_, lbk], [sk_])
                    self.ts(kf, sa, -1.0, 1.0, ALU.mult, ALU.add, [sk_], [kfk])
                    self.act(logf, sa, AF.Ln, [sk_], [logfk])
                    gb = [self.bank(0), self.bank(1)]
                    rb = [self.bank(2), self.bank(3)]
                    for c in range(2):
                        cs = slice(c * 512, (c + 1) * 512)
                        self.mm(gb[c][0], self.cs["linc_%d" % d][:], logf[:, cs], True, True, ["cs_linc_%d" % d, logfk], [gb[c][1]])
                        self.mm(rb[c][0], self.cs["lrem_%d" % d][:], logf[:, cs], True, True, ["cs_lrem_%d" % d, logfk], [rb[c][1]])
                    tb, tbk = self.bank(4)
                    for h in range(8):
                        self.mm(tb[:, h * 2:h * 2 + 2], logf[:, h * 128:(h + 1) * 128], self.cs["csel"][:], True, True, [logfk, "cs_csel"], [tbk])
                    eta, etk = etot[d]
                    self.act(eta, tb[:, 0:16], AF.Exp, [tbk], [etk])
                    qta, qtk = qt[d]
                    kta, ktk = kt[d]
                    kha, khk = kh[d]
                    for c in range(2):
                        cs = slice(c * 512, (c + 1) * 512)
                        self.act(E[0][0][:, cs], gb[c][0], AF.Exp, [gb[c][1]], [E[0][1]])
                        self.act(E[1][0][:, cs], gb[c][0], AF.Exp, [gb[c][1]], [E[1][1]], scale=-1.0)
                    self.tt(qta, qs, E[0][0], ALU.mult, [qsk, E[0][1]], [qtk])
                    self.tt(kta, kf, E[1][0], ALU.mult, [kfk, E[1][1]], [ktk])
                    for c in range(2):
                        cs = slice(c * 512, (c + 1) * 512)
                        self.act(E[0][0][:, cs], rb[c][0], AF.Exp, [rb[c][1]], [E[0][1]])
                    self.tt(kha, kf, E[0][0], ALU.mult, [kfk, E[0][1]], [khk])
                    self.transpose8(qta, qtk, qT[d][0], qT[d][1], 1)
                    self.transpose8(kta, ktk, kT[d][0], kT[d][1], 0)
                    qTv = qT[d][0].rearrange("p (h t) -> p h t", h=8)
                    kTv = kT[d][0].rearrange("p (h t) -> p h t", h=8)
                    for hg in range(2):
                        sb, sbk = nb()
                        for hh in range(4):
                            h = hg * 4 + hh
                            self.mm(sb[:, hh * 128:(hh + 1) * 128], kTv[:, h, :], qTv[:, h, :], True, True, [kT[d][1], qT[d][1]], [sbk])
                        self.tt(ATv[:, hg * 4:(hg + 1) * 4, :], sb.rearrange("p (h t) -> p h t", h=4),
                                self.cs["maski_%d" % d][:].unsqueeze(1).to_broadcast([128, 4, 128]), ALU.mult, [sbk, "cs_maski_%d" % d], [ATk])
                    if d == 0:
                        recur_tile(0, qT[0][0], qT[0][1], kha, khk, V, Vk, eta, etk, False, True, ATv, ATk, "set")
                    else:
                        for hg in range(2):
                            ob, obk = self.bank(4 + hg)
                            for hh in range(4):
                                h = hg * 4 + hh
                                self.mm(ob[:, hh * 128:(hh + 1) * 128], ATv[:, h, :], V[:, h * 128:(h + 1) * 128], True, True, [ATk, Vk], [obk])
                            cs = slice(hg * 512, (hg + 1) * 512)
                            self.tt(oacc[:, cs], oacc[:, cs], ob, ALU.add, [obk, oacck], [oacck])
                        self.store(self.QT[g], qT[1][0], [qT[1][1]], "hqt")
                        self.store(self.KH[r0:r0 + 128, :], kha, [khk], "hkh")
                        self.store(self.ETOT[g], eta, [etk], "hetot")
                if li == 3 and g in (0, 15):
                    self.dump("h1_x_%d" % g, x32, xk, [128, D])
                    self.dump("h1_qs_%d" % g, qs, qsk, [128, D])
                    self.dump("h1_lb_%d" % g, lb, lbk, [128, D])
                    self.dump("h1_sig0_%d" % g, sig[0][0], sig[0][1], [128, D])
                    self.dump("h1_E0_%d" % g, E[0][0], E[0][1], [128, D])
                    self.dump("h1_S32_%d" % g, S32, [S32k + str(h) for h in range(8)], [128, D])
                    self.dump("h1_oacc_%d" % g, oacc, oacck, [128, D])
                self.store(self.OACC[r0:r0 + 128, :], oacc, [oacck], "hoacc")
        self.reset()
        wo, wok = self.a16(KC * D, "hwo")
        wov = wo.rearrange("p (k c) -> p k c", k=KC)
        for k in range(KC):
            self.load(wov[:, k, :], self.w["hg_w_o"][j][k * 128:(k + 1) * 128, :], [wok], "w_hwo", eng="pool")
        gn, gnk = self.bcload("gnorm", self.w["hg_gnorm"][j])
        x32, xk = self.a32(D, "x")
        junk, junkk = self.a32(D, "junk")
        st, stk = self.a32(32, "st")
        sgt, sgtk = self.a32(D, "sgate")
        S32, S32k = self.a32(D, "S32")
        oacc, oacck = self.a32(D, "oacc")
        eta, etk = self.a32(16, "etot")
        xo, xok = self.a32(D, "xo")
        qTa, qTk = self.a16(D, "qT")
        kha, khk = self.a16(D, "kh")
        V, Vk = self.a16(D, "V")
        Sbf, Sbfk = self.a16(D, "Sbf")
        on, onk = self.a16(D, "on")
        onT, onTk = self.a16(D, "onT")
        onTv = onT.rearrange("p (k t) -> p k t", k=KC)
        for tiles in self.seq_tiles:
            zero_state()
            for g in reversed(tiles):
                r0 = g * 128
                self.load(qTa, self.QT[g], [qTk], "h2qt")
                self.load(kha, self.KH[r0:r0 + 128, :], [khk], "h2kh")
                self.load(V, self.VV[r0:r0 + 128, :], [Vk], "h2vv")
                self.load(eta, self.ETOT[g], [etk], "h2et")
                self.load(oacc, self.OACC[r0:r0 + 128, :], [oacck], "h2oa")
                self.load(sgt, self.GATE[r0:r0 + 128, :], [sgtk], "h2g")
                self.load(x32, self.XS[r0:r0 + 128, :], [xk, "XS%d" % g], "h2x")
                if g == 0 and li == 3:
                    self.dump("h2_oacc_in", oacc, oacck, [128, D])
                    self.dump("h2_eta", eta, etk, [128, 16])
                    self.dump("h2_S32_in", S32, [S32k + str(h) for h in range(8)], [128, D])
                    self.dump("h2_qT", qTa, qTk, [128, D], BF16)
                    self.dump("h2_kh", kha, khk, [128, D], BF16)
                recur_tile(1, qTa, qTk, kha, khk, V, Vk, eta, etk, False, False, None, None, "add")
                if g == 0:
                    self.dump("h2_oacc_out", oacc, oacck, [128, D])
                self.act(junk, oacc, AF.Square, [oacck], [junkk])
                self.reduce(st[:, 0:8], junk.rearrange("p (h v) -> p h v", h=8), [junkk], [stk])
                self.ts(st[:, 8:16], st[:, 0:8], 1.0 / 128, NORM_EPS, ALU.mult, ALU.add, [stk], [stk])
                self.act(st[:, 16:24], st[:, 8:16], AF.Sqrt, [stk], [stk])
                self.recip(st[:, 24:32], st[:, 16:24], [stk], [stk])
                self.tt(oacc.rearrange("p (h v) -> p h v", h=8), oacc.rearrange("p (h v) -> p h v", h=8),
                        st[:, 24:32].unsqueeze(2).to_broadcast([128, 8, 128]), ALU.mult, [oacck, stk], [oacck])
                self.tt(oacc, oacc, gn, ALU.mult, [oacck, gnk], [oacck])
                self.tt(on, oacc, sgt, ALU.mult, [oacck, sgtk], [onk])
                self.transpose8(on, onk, onT, onTk, 0)
                for cg in range(2):
                    pb, pk = nb()
                    for k in range(KC):
                        self.mm(pb, onTv[:, k, :], wov[:, k, cg * 512:(cg + 1) * 512], k == 0, k == KC - 1, [onTk, wok], [pk])
                    self.tt(xo[:, cg * 512:(cg + 1) * 512], x32[:, cg * 512:(cg + 1) * 512], pb, ALU.add, [xk, pk], [xok])
                if li == 3 and g in (0, 15):
                    self.dump("h2_xo_%d" % g, xo, xok, [128, D])
                self.store(self.XS[r0:r0 + 128, :], xo, [xok], "h2xo", writes=["XS%d" % g])

    def copy_in(self):
        self.reset()
        xt = [self.a32(4 * D, "cx%d" % i) for i in range(2)]
        nsup = (self.NT + 3) // 4
        for su in range(nsup):
            r0 = su * 512
            n = min(self.T - r0, 512)
            xa, xak = xt[su % 2]
            xav = xa.rearrange("p (i c) -> p i c", i=4)
            self.load(xav[:, 0:n // 128, :], self.x_in[r0:r0 + n, :].rearrange("(i p) c -> p i c", p=128), [xak], "cx%d" % (su % 2))
            self.store(self.XS[r0:r0 + n, :].rearrange("(i p) c -> p i c", p=128), xav[:, 0:n // 128, :], [xak], "cxs%d" % (su % 2))

    def final_phase(self):
        self.reset()
        nw, nwk = self.bcload("nfin", self.w["norm_final"])
        xt = [self.a32(D, "fx%d" % i) for i in range(2)]
        ot = [self.a32(D, "fo%d" % i) for i in range(2)]
        st, stk = self.a32(4, "st")
        junk, junkk = self.a32(D, "junk")
        for g in range(self.NT):
            xa, xak = xt[g % 2]
            oa, oak = ot[g % 2]
            self.load(xa, self.XS[g * 128:(g + 1) * 128, :], [xak], "fx%d" % (g % 2))
            if self.final_norm:
                self.rmsnorm_tile(xa, xak, nw, nwk, oa, oak, junk, junkk, st, stk)
            else:
                self.copy(oa, xa, [xak], [oak])
            self.store(self.y_out[g * 128:(g + 1) * 128, :], oa, [oak], "fo%d" % (g % 2))

    def build(self):
        nc = self.nc
        self.XS3 = nc.dram_tensor("XS3", [self.T, D], F32).ap()
        self.load_consts()
        self.copy_in()
        ri = 0
        hi = 0
        for li, kind in enumerate(self.layers):
            if kind == "rwkv":
                from_rw = getattr(self, "rwkv_layer", None)
                if from_rw is not None:
                    from_rw(li, li // 2)
            elif kind == "hgrn":
                from_hg = getattr(self, "hgrn_layer", None)
                if from_hg is not None:
                    from_hg(li, li // 2)
            elif kind == "ffn":
                pass
            if kind != "none":
                self.ffn_layer(li)
        self.final_phase()
        self.S.finalize()
        return nc


def build_program(seq_lens, layers, final_norm=True, debug=False):
    b = Builder(seq_lens, layers, final_norm)
    b.debug = debug
    import os
    b.stop_after = os.environ.get("RW_STOP", "")
    b.cut = int(os.environ.get("RW_CUT", "0"))
    nc = b.build()
    return nc, b


def run_cores(nc, x_cores, weights):
    consts = _consts()
    in_maps = []
    for xc in x_cores:
        m = {"x": np.ascontiguousarray(xc, dtype=np.float32)}
        for k in WEIGHT_SHAPES:
            m[k] = np.ascontiguousarray(weights[k], dtype=np.float32)
        for k, v in consts.items():
            m["c_" + k] = v
        in_maps.append(m)
    res = run_bass_kernel_spmd(nc, in_maps, core_ids=list(range(len(x_cores))))
    global LAST_RESULTS
    LAST_RESULTS = res.results
    return [r["y"] for r in res.results]


def kernel(**inputs):
    xp = np.asarray(inputs["x_prompt"], dtype=np.float32)
    xs = np.asarray(inputs["x_sample"], dtype=np.float32)
    seq_lens = [2048, 2048, 8192]
    nc, _ = build_program(seq_lens, ["rwkv", "hgrn", "rwkv", "hgrn"])
    x_cores = [np.concatenate([xp[2 * c], xp[2 * c + 1], xs[c]], axis=0) for c in range(NCORES)]
    weights = {k: np.asarray(inputs[k], dtype=np.float32) for k in WEIGHT_SHAPES}
    ys = run_cores(nc, x_cores, weights)
    y_prompt = np.stack([ys[c // 2][(c % 2) * 2048:(c % 2 + 1) * 2048] for c in range(16)], axis=0)
    y_sample = np.stack([ys[c][4096:] for c in range(NCORES)], axis=0)
    return (y_prompt.astype(np.float32), y_sample.astype(np.float32))
```
